# Optimizing a Trainium2 kernel written in Bass

```python
import jax, jax.numpy as jnp
from jax import lax
import numpy as np


D_MODEL = 2048
BATCH = 8
SEQ = 2048
DEPTH = 1

MIX_WIDTH = D_MODEL
GROUP_DIM = 128
FOURIER_WIDTH = MIX_WIDTH // 2
GMLP_WIDTH = MIX_WIDTH - FOURIER_WIDTH
N_FOURIER_GROUPS = FOURIER_WIDTH // GROUP_DIM
N_GMLP_HEADS = GMLP_WIDTH // GROUP_DIM
IN_PROJ_WIDTH = FOURIER_WIDTH + 2 * GMLP_WIDTH
CHUNK = 128
D_FF = 4 * D_MODEL
EPS = 1e-6

kernel_name = "hybrid_fourier_gmlp_encoder_block"


def rmsnorm(x, g):
    xf = x.astype(jnp.float32)
    y = xf * lax.rsqrt(jnp.mean(xf * xf, axis=-1, keepdims=True) + EPS)
    return (y * g.astype(jnp.float32)).astype(x.dtype)


def fourier_groups(a, w_f):
    B, S, _ = a.shape
    a4 = a.reshape(B, S, N_FOURIER_GROUPS, GROUP_DIM).astype(jnp.float32)
    f = jnp.real(jnp.fft.fft2(a4, axes=(1, 3), norm="ortho")).astype(a.dtype)
    y = jnp.einsum('bsgc,gcd->bsgd', f, w_f)
    return y.reshape(B, S, FOURIER_WIDTH)


def gmlp_groups(z, g_v, w_s, b_s):
    B, S, _ = z.shape
    z = jax.nn.gelu(z, approximate=False)
    u, v = z[..., :GMLP_WIDTH], z[..., GMLP_WIDTH:]
    v = rmsnorm(v.reshape(B, S, N_GMLP_HEADS, GROUP_DIM), g_v)
    v = v.reshape(B, S // CHUNK, CHUNK, N_GMLP_HEADS, GROUP_DIM)
    s = jnp.einsum('hpq,bnqhd->bnphd', w_s, v) + b_s.T[None, None, :, :, None]
    return u * s.reshape(B, S, GMLP_WIDTH)


def setup_inputs(seed: int = 0) -> dict:
    key = jax.random.key(seed)
    ks = jax.random.split(key, 14)
    f32 = jnp.float32
    x = jax.random.normal(ks[0], (BATCH, SEQ, D_MODEL), f32)
    norm_mix_g = 1.0 + 0.05 * jax.random.normal(ks[1], (D_MODEL,), f32)
    w_in = jax.random.normal(ks[2], (D_MODEL, IN_PROJ_WIDTH), f32) * D_MODEL ** -0.5
    fourier_w = jax.random.normal(ks[3], (N_FOURIER_GROUPS, GROUP_DIM, GROUP_DIM), f32) * GROUP_DIM ** -0.5
    gmlp_v_g = 1.0 + 0.05 * jax.random.normal(ks[4], (N_GMLP_HEADS, GROUP_DIM), f32)
    gmlp_ws = jax.random.normal(ks[5], (N_GMLP_HEADS, CHUNK, CHUNK), f32) * CHUNK ** -0.5
    gmlp_b = 1.0 + 0.01 * jax.random.normal(ks[6], (N_GMLP_HEADS, CHUNK), f32)
    w_out = jax.random.normal(ks[7], (MIX_WIDTH, D_MODEL), f32) * MIX_WIDTH ** -0.5
    norm_mlp_g = 1.0 + 0.05 * jax.random.normal(ks[8], (D_MODEL,), f32)
    w_up = jax.random.normal(ks[9], (D_MODEL, D_FF), f32) * D_MODEL ** -0.5
    w_down = jax.random.normal(ks[10], (D_FF, D_MODEL), f32) * D_FF ** -0.5
    norm_final_g = 1.0 + 0.05 * jax.random.normal(ks[11], (D_MODEL,), f32)
    return {"x": x, "norm_mix_g": norm_mix_g, "w_in": w_in, "fourier_w": fourier_w,
            "gmlp_v_g": gmlp_v_g, "gmlp_ws": gmlp_ws, "gmlp_b": gmlp_b, "w_out": w_out,
            "norm_mlp_g": norm_mlp_g, "w_up": w_up, "w_down": w_down,
            "norm_final_g": norm_final_g}


def reference(x, norm_mix_g, w_in, fourier_w, gmlp_v_g, gmlp_ws, gmlp_b, w_out,
              norm_mlp_g, w_up, w_down, norm_final_g):
    h = x
    for _ in range(DEPTH):
        p = jnp.einsum('bsd,de->bse', rmsnorm(h, norm_mix_g), w_in)
        y_f = fourier_groups(p[..., :FOURIER_WIDTH], fourier_w)
        y_g = gmlp_groups(p[..., FOURIER_WIDTH:], gmlp_v_g, gmlp_ws, gmlp_b)
        mix = jnp.concatenate([y_f, y_g], axis=-1)
        h = h + jnp.einsum('bse,ed->bsd', mix, w_out)
        a = jnp.einsum('bsd,df->bsf', rmsnorm(h, norm_mlp_g), w_up)
        h = h + jnp.einsum('bsf,fd->bsd', jnp.square(jax.nn.relu(a)), w_down)
    return rmsnorm(h, norm_final_g)
```

```python
import math
from contextlib import ExitStack

import ml_dtypes
import numpy as np

import concourse.bass as bass
import concourse.mybir as mybir
from concourse.bass_utils import run_bass_kernel_spmd

F32 = mybir.dt.float32
BF16 = mybir.dt.bfloat16
U8 = mybir.dt.uint8
AF = mybir.ActivationFunctionType
ALU = mybir.AluOpType
AX = mybir.AxisListType

D = 2048
S = 2048
NCH = 16
DFF = 8192
EPS = 1e-6
NCORES = 8
NSLOT = 8

R0 = 0
R1 = 65536
R2 = 98304
R3 = 131072
R4 = 163840
R5 = 196608
CVEC_OFF = R5
ONES_OFF = R5 + 224
ONEV_OFF = R5 + 480
RSTD_OFF = R5 + 512
S5 = R5 + 1024
ARENA = 212736
S5_SIZE = ARENA - S5


class Tracker:
    def __init__(self):
        self.prog = {}
        self.count = {}
        self.seen = {}
        self.state = {}
        self.region_prev = {}
        self.engsem = {}

    def add_engine(self, eng, semkey):
        self.prog[eng] = []
        self.seen[eng] = {}
        self.engsem[eng] = semkey
        self.count.setdefault(semkey, 0)

    def _st(self, key):
        if key not in self.state:
            self.state[key] = {"w": {}, "r": {}}
        return self.state[key]

    def barrier(self, region):
        prev = self.region_prev.setdefault(region, {})
        for key, st in self.state.items():
            if key[0] == region:
                for d in (st["w"], st["r"]):
                    for s, v in d.items():
                        if prev.get(s, 0) < v:
                            prev[s] = v

    def emit(self, eng, fn, reads=(), writes=(), dma_sem=None, inc=True):
        deps = {}

        def add(d):
            for s, v in d.items():
                if deps.get(s, 0) < v:
                    deps[s] = v

        own = self.engsem[eng]
        for k in reads:
            add(self._st(k)["w"])
        for k in writes:
            st = self._st(k)
            skip = own if eng == "pe" else None
            add({s: v for s, v in st["w"].items() if s != skip})
            add({s: v for s, v in st["r"].items() if s != skip})
            add({s: v for s, v in self.region_prev.get(k[0], {}).items() if s != skip})
        seen = self.seen[eng]
        for s, v in deps.items():
            if seen.get(s, 0) < v:
                self.prog[eng].append(("wait", s, v))
                seen[s] = v
        if dma_sem is not None:
            self.count.setdefault(dma_sem, 0)
            self.count[dma_sem] += 16
            tok = (dma_sem, self.count[dma_sem])
            self.prog[eng].append(("op", fn, dma_sem, 16))
        elif inc:
            self.count[own] += 1
            tok = (own, self.count[own])
            self.prog[eng].append(("op", fn, own, 1))
        else:
            tok = None
            self.prog[eng].append(("op", fn, None, 0))
        if tok is not None:
            for k in reads:
                d = self._st(k)["r"]
                if d.get(tok[0], 0) < tok[1]:
                    d[tok[0]] = tok[1]
            for k in writes:
                d = self._st(k)["w"]
                if d.get(tok[0], 0) < tok[1]:
                    d[tok[0]] = tok[1]
        return tok


def _check_no_deadlock(tr):
    pc = {e: 0 for e in tr.prog}
    sem = {}
    progress = True
    while progress:
        progress = False
        for e, items in tr.prog.items():
            while pc[e] < len(items):
                it = items[pc[e]]
                if it[0] == "wait":
                    if sem.get(it[1], 0) < it[2]:
                        break
                else:
                    if it[2] is not None:
                        sem[it[2]] = sem.get(it[2], 0) + it[3]
                pc[e] += 1
                progress = True
    stuck = {e: (pc[e], len(items)) for e, items in tr.prog.items() if pc[e] < len(items)}
    assert not stuck, f"deadlock in recorded program: {stuck}"
    for k, v in tr.count.items():
        assert sem.get(k, 0) == v, (k, sem.get(k, 0), v)


def build_nc(debug=False):
    nc = bass.Bass("TRN2", target_bir_lowering=False)
    dbg_outs = []
    xT = nc.dram_tensor("xT", [D, S], F32, kind="ExternalInput").ap()
    wfm = nc.dram_tensor("wfm", [16, 128, 2048], F32, kind="ExternalInput").ap()
    wv = nc.dram_tensor("wv", [2, 128, 8192], F32, kind="ExternalInput").ap()
    wout = nc.dram_tensor("wout", [16, 128, 2048], F32, kind="ExternalInput").ap()
    wup = nc.dram_tensor("wup", [64, 128, 2048], F32, kind="ExternalInput").ap()
    wdn = nc.dram_tensor("wdn", [64, 128, 2048], F32, kind="ExternalInput").ap()
    cvec_d = nc.dram_tensor("cvec", [128, 56], F32, kind="ExternalInput").ap()
    gb_d = nc.dram_tensor("gb", [8, 128], F32, kind="ExternalInput")
    wsT_d = nc.dram_tensor("wsT", [128, 1024], F32, kind="ExternalInput").ap()
    wf_d = nc.dram_tensor("wf", [128, 1024], F32, kind="ExternalInput").ap()
    ccsc_d = nc.dram_tensor("ccsc", [128, 272], BF16, kind="ExternalInput").ap()
    tabs_d = nc.dram_tensor("tabs", [2, 128, 16384], BF16, kind="ExternalInput").ap()
    yT = nc.dram_tensor("yT", [D, S], F32, kind="ExternalOutput").ap()

    xT_v = xT.rearrange("(c p) t -> p c t", p=128)
    yT_v = yT.rearrange("(c p) t -> p c t", p=128)

    tr = Tracker()
    for eng in ("pe", "act", "dve", "pool", "sp"):
        tr.add_engine(eng, eng)

    with ExitStack() as es:
        arena = es.enter_context(nc.sbuf_tensor("arena", [128, ARENA], U8))
        psum = [es.enter_context(nc.psum_tensor(f"ps{i}", [128, 512], F32)) for i in range(8)]

        def view(off, nbytes, dt, pat=None, **kw):
            v = arena[:, off:off + nbytes].bitcast(dt)
            if pat is not None:
                v = v.rearrange(pat, **kw)
            return v

        cvec = view(CVEC_OFF, 224, F32)
        ones = view(ONES_OFF, 256, BF16)
        epsb = view(ONEV_OFF, 4, F32)
        rstd_all = view(RSTD_OFF, 512, F32)
        xn = view(R0, 65536, BF16, "p (c t) -> p c t", t=S)
        Abuf = view(R0, 65536, BF16, "p (t g m) -> p t g m", g=8, m=256)
        hbuf = view(R0, 65536, F32, "p (c t) -> p c t", t=1024)
        mix = [view(R1, 32768, BF16, "p (c t) -> p c t", t=1024),
               view(R2, 32768, BF16, "p (c t) -> p c t", t=1024)]
        rbuf = view(R1, 32768, BF16, "p (c t) -> p c t", t=1024)
        xs = [view(R3 + i * 16384, 16384, F32, "p (c t) -> p c t", t=256) for i in range(2)]
        vtok = view(R3, 32768, BF16, "p (t e) -> p t e", e=1024)
        hn = view(R3, 32768, BF16, "p (c t) -> p c t", t=1024)
        tab = [view(R3, 32768, BF16, "p (a s k) -> p a s k", a=2, k=512),
               view(R4, 32768, BF16, "p (a s k) -> p a s k", a=2, k=512)]
        wslot = [view(R4 + i * 4096, 4096, BF16, "p (c j) -> p c j", j=128) for i in range(NSLOT)]
        wvslot = [view(R4 + i * 16384, 16384, BF16, "p (c j) -> p c j", j=512) for i in range(2)]

        bank_ctr = [0]
        reserved = set()

        def next_bank():
            while True:
                b = bank_ctr[0] % 8
                bank_ctr[0] += 1
                if b not in reserved:
                    return b

        def dma(eng, out, in_, sem, reads=(), writes=()):
            tr.emit(eng, lambda h, o=out, i=in_: h.dma_start(out=o, in_=i),
                    reads=reads, writes=writes, dma_sem=sem)

        def dump(name, src):
            if not debug:
                return
            shp = list(src.shape)
            dt = nc.dram_tensor("dbg_" + name, shp, src.dtype, kind="ExternalOutput").ap()
            dbg_outs.append("dbg_" + name)
            sk = "d_dbg_" + name
            tr.count[sk] = 16
            for s_, v_ in tr.count.items():
                if s_ != sk and tr.seen["sp"].get(s_, 0) < v_:
                    tr.prog["sp"].append(("wait", s_, v_))
                    tr.seen["sp"][s_] = v_
            tr.prog["sp"].append(("op", lambda h, o=dt, i=src: h.dma_start(out=o, in_=i), sk, 16))
            for e in tr.prog:
                tr.prog[e].append(("wait", sk, 16))
                tr.seen[e][sk] = 16

        def mm_group(bank, mms, reads, extra_writes=()):
            n = len(mms)

            def fn(h, mms=mms):
                last = None
                for (o, l, r, st, sp) in mms:
                    last = h.matmul(o, l, r, start=st, stop=sp)
                return last
            tr.emit("pe", fn, reads=reads, writes=[("PS", bank)] + list(extra_writes))

        wq = []
        wq.append(("v", wv[0]))
        wq.append(("v", wv[1]))
        EB_ORDER = list(range(8, 16)) + list(range(8))
        for eb in EB_ORDER:
            wq.append(("std", wfm[eb]))
        for t in range(2):
            for db in range(16):
                wq.append(("std", wout[db]))
            for q in range(4):
                for fc in range(16):
                    wq.append(("std", wup[q * 16 + fc]))
                for rep in range(2 if (t == 1 and q == 3) else 1):
                    for db in range(16):
                        wq.append(("std", wdn[q * 16 + db]))
        w_issued = [0]
        w_limit = [1]
        w_slot_of = {}
        slot_ctr = [0]

        slot_tile = [-1] * NSLOT
        w_extra_reads = []

        xslot = [view(R1 + i * 4096, 4096, BF16, "p (c j) -> p c j", j=128) for i in range(4)]
        XT0 = 2

        def wtile(sid):
            return wslot[sid] if sid < 100 else xslot[sid - 100]

        def wkey(sid):
            return ("R4", "w", sid) if sid < 100 else ("R1", "wx", sid - 100)

        def issue_x_tiles():
            for j in range(4):
                i = XT0 + j
                assert w_issued[0] == i
                dma("pool", xslot[j].rearrange("p c j -> p (c j)"), wq[i][1], f"d_wx{j}",
                    writes=[("R1", "wx", j)])
                w_slot_of[i] = 100 + j
                w_issued[0] += 1

        def issue_weights(upto, cur):
            while w_issued[0] <= min(upto, len(wq) - 1, w_limit[0]):
                i = w_issued[0]
                kind, src = wq[i]
                if kind == "std":
                    s = slot_ctr[0] % NSLOT
                    need = [s]
                    adv = 1
                else:
                    pad = (-slot_ctr[0]) % 4
                    s = (slot_ctr[0] + pad) % NSLOT
                    need = [s + j for j in range(4)]
                    adv = pad + 4
                if any(slot_tile[q] >= cur for q in need):
                    break
                slot_ctr[0] += adv
                for q in need:
                    slot_tile[q] = i
                if kind == "std":
                    dst = wslot[s].rearrange("p c j -> p (c j)")
                else:
                    dst = wvslot[s // 4].rearrange("p c j -> p (c j)")
                keys = [("R4", "w", q) for q in need]
                w_slot_of[i] = s
                dma("pool", dst, src, f"d_w{s}", reads=list(w_extra_reads), writes=keys)
                w_issued[0] += 1

        LOOK = 6
        w_next = [0]

        def take_weight():
            i = w_next[0]
            w_next[0] += 1
            issue_weights(i + LOOK, i)
            assert i in w_slot_of, i
            return w_slot_of[i]

        dma("sp", cvec, cvec_d, "d_c0", writes=[("C", "cvec")])
        tr.emit("dve", lambda h: h.memset(ones, 1.0 / D), writes=[("C", "ones")])
        tr.emit("dve", lambda h: h.memset(epsb, EPS), writes=[("C", "eps")])
        w_limit[0] = 0
        sv0 = take_weight()
        wvkeys = [[("R4", "w", sv0 + j) for j in range(4)], None]

        xs = [view(R1 + i * 16384, 16384, F32, "p (c t) -> p c t", t=256) for i in range(2)]
        rs_a = [view(S5 + i * 1024, 1024, F32) for i in range(2)]
        sq16 = view(S5 + 2048, 8192, BF16, "p (c t) -> p c t", t=256)
        sqv = [view(S5 + 10240 + i * 2048, 2048, F32) for i in range(2)]
        ss_all = view(S5 + 14336, 512, F32)
        it = 0

        def a0_sq(tb):
            sl = tb % 2
            tsl = slice(tb * 256, (tb + 1) * 256)
            dma("sp", xs[sl], xT_v[:, :, tsl], f"d_xs{sl}", reads=(wvkeys[0] if tb == 1 else []),
                writes=[("R1", "xs", sl)])
            tr.emit("act", lambda h, sl=sl: h.activation(out=sq16, in_=xs[sl], func=AF.Square),
                    reads=[("R1", "xs", sl)], writes=[("S5", "sq16")])

        def a0_rest(tb):
            sl = tb % 2
            tsl = slice(tb * 256, (tb + 1) * 256)
            b = next_bank()
            mm_group(b, [(psum[b][:, 0:256], ones, sq16[:, c, :], c == 0, c == 15) for c in range(16)],
                     reads=[("S5", "sq16"), ("C", "ones")])
            tr.emit("act", lambda h, b=b, sl=sl: h.activation(
                out=rs_a[sl], in_=psum[b][:, 0:256], func=AF.Sqrt, bias=epsb[:, 0:1], scale=1.0),
                reads=[("PS", b), ("C", "eps")], writes=[("S5", "rs", sl)])
            tr.emit("dve", lambda h, sl=sl: h.reciprocal(out=rs_a[sl], in_=rs_a[sl]),
                    reads=[("S5", "rs", sl)], writes=[("S5", "rs", sl)])
            for c in range(16):
                tr.emit("dve", lambda h, c=c, sl=sl, tsl=tsl: h.scalar_tensor_tensor(
                    out=xn[:, c, tsl], in0=xs[sl][:, c, :], scalar=cvec[:, c:c + 1], in1=rs_a[sl],
                    op0=ALU.mult, op1=ALU.mult),
                    reads=[("R1", "xs", sl), ("S5", "rs", sl), ("C", "cvec")],
                    writes=[("R0", "xn", tb, c)])

        def a2_block(tb):
            nonlocal_it = a2_it
            for cb in range(2):
                for tt in (2 * tb, 2 * tb + 1):
                    b = next_bank()
                    k = nonlocal_it[0] % 2
                    nonlocal_it[0] += 1
                    mm_group(b, [(psum[b][:, :], xn[:, c, tt * 128:(tt + 1) * 128], wvv[cb][:, c, :], c == 0, c == 15)
                                 for c in range(16)],
                             reads=wvkeys[cb] + [("R0", "xn", tb, c) for c in range(16)])
                    vblk = vtok[:, tt, cb * 512:(cb + 1) * 512]
                    tr.emit("act", lambda h, b=b, vblk=vblk: h.activation(out=vblk, in_=psum[b][:, :], func=AF.Gelu),
                            reads=[("PS", b)], writes=[("R3", "v", tt, cb)])
                    tr.emit("dve", lambda h, k=k, vblk=vblk: h.tensor_tensor(out=sqv[k], in0=vblk, in1=vblk, op=ALU.mult),
                            reads=[("R3", "v", tt, cb)], writes=[("S5", "sqv", k)])
                    c0 = tt * 8 + cb * 4
                    tr.emit("dve", lambda h, k=k, c0=c0: h.tensor_reduce(
                        out=ss_all[:, c0:c0 + 4], in_=sqv[k].rearrange("p (a d) -> p a d", d=128),
                        axis=AX.X, op=ALU.add),
                        reads=[("S5", "sqv", k)], writes=[("S5", "ss_all", c0)])

        a2_it = [0]
        a0_sq(0)
        w_limit[0] = 1
        w_extra_reads.append(("R1", "xs", 0))
        sv1 = take_weight()
        w_extra_reads.clear()
        wvkeys[1] = [("R4", "w", sv1 + j) for j in range(4)]
        wvv = [wvslot[sv0 // 4], wvslot[sv1 // 4]]
        a0_rest(0)
        a0_sq(1)
        for tb in range(8):
            if tb + 1 < 8:
                a0_rest(tb + 1)
                if tb + 1 == 7:
                    tr.barrier("R1")
                    issue_x_tiles()
            if tb + 2 < 8:
                a0_sq(tb + 2)
            a2_block(tb)
        tr.emit("act", lambda h: h.activation(out=rstd_all, in_=ss_all, func=AF.Sqrt, bias=epsb[:, 0:1],
                                              scale=1.0 / 128.0),
                reads=[("S5", "ss_all", tt * 8 + cb * 4) for tt in range(16) for cb in range(2)] + [("C", "eps")],
                writes=[("C", "rstd_all")])
        tr.emit("dve", lambda h: h.reciprocal(out=rstd_all, in_=rstd_all),
                reads=[("C", "rstd_all")], writes=[("C", "rstd_all")])
        w_limit[0] = 17
        dump("xn", xn)
        dump("vtok", vtok)
        dump("rstd", rstd_all)

        tr.barrier("R1")
        tr.barrier("S5")
        bbc = view(S5, 4096, F32, "p (h q) -> p h q", q=128)
        wsT = view(S5 + 4096, 2048, BF16, "p (h q) -> p h q", q=128)
        tb3 = [view(S5 + 6144, 2048, F32)] * 2
        wfb = view(S5 + 8192, 2048, BF16, "p (g d) -> p g d", d=128)
        ccsc = view(S5 + 14336, 544, BF16)
        Mg = view(S5, 4096, BF16, "p (g m) -> p g m", m=256)
        dma("sp", ccsc, ccsc_d, "d_c3", writes=[("S5", "ccsc")])
        dma("pool", wfb.rearrange("p g d -> p (g d)"), wf_d, "d_c4", writes=[("S5", "wfb")])
        NWS = 16
        wsS = [view(S5 + 10240 + i * 256, 256, BF16) for i in range(NWS)]
        dma("sp", bbc, bass.AP(gb_d, 0, [[0, 128], [128, 8], [1, 128]]), "d_c1", writes=[("S5", "bbc")])
        dma("pool", wsT.rearrange("p h q -> p (h q)"), wsT_d, "d_c2", writes=[("S5", "wsT")])
        b3_it = [0]
        b3_iw = [0]

        def gmlp_scale(hd):
            for tt in range(16):
                col = tt * 8 + hd
                tr.emit("act", lambda h, tt=tt, hd=hd, col=col: h.activation(
                    out=wsS[tt], in_=wsT[:, hd, :], func=AF.Copy, scale=rstd_all[:, col:col + 1]),
                    reads=[("S5", "wsT"), ("C", "rstd_all")], writes=[("S5", "wsS", tt)])

        def gmlp_head(hd):
            for tb in range(4):
                b = next_bank()
                k = 0
                mms = []
                rd = []
                for j in range(4):
                    tt = tb * 4 + j
                    kw = tt
                    mms.append((psum[b][:, j * 128:(j + 1) * 128], vtok[:, tt, hd * 128:(hd + 1) * 128],
                                wsS[kw], True, True))
                    rd += [("S5", "wsS", kw), ("R3", "v", tt, hd // 4)]
                mm_group(b, mms, reads=rd)
                th, off = tb // 2, (tb % 2) * 512
                ublk = mix[th][:, 8 + hd, off:off + 512]
                key = ("R1" if th == 0 else "R2", "mix", 8 + hd, tb % 2)
                tr.emit("dve", lambda h, b=b, k=k, hd=hd: h.scalar_tensor_tensor(
                    out=tb3[k].rearrange("p (a q) -> p a q", q=128),
                    in0=psum[b][:, :].rearrange("p (a q) -> p a q", q=128),
                    scalar=cvec[:, 48 + hd:49 + hd],
                    in1=bbc[:, hd, :].unsqueeze(1).to_broadcast([128, 4, 128]),
                    op0=ALU.mult, op1=ALU.add),
                    reads=[("PS", b), ("S5", "bbc"), ("C", "cvec")], writes=[("S5", "t3", k)])
                tr.emit("dve", lambda h, k=k, ublk=ublk: h.tensor_tensor(
                    out=ublk, in0=tb3[k], in1=ublk, op=ALU.mult),
                    reads=[("S5", "t3", k), key], writes=[key])

        for i, eb in enumerate(EB_ORDER):
            s = take_weight()
            if i == 8:
                tr.barrier("R1")
            if 1 <= i <= 8:
                gmlp_scale(i - 1)
            for tb in range(4):
                b = next_bank()
                tsl = slice(tb * 512, (tb + 1) * 512)
                mm_group(b, [(psum[b][:, :], wtile(s)[:, c, :], xn[:, c, tsl], c == 0, c == 15)
                             for c in range(16)],
                         reads=[wkey(s)] + [("R0", "xn", 2 * tb + u, c) for u in range(2) for c in range(16)])
                th, off = tb // 2, (tb % 2) * 512
                dst = mix[th][:, eb, off:off + 512]
                key = ("R1" if th == 0 else "R2", "mix", eb, tb % 2)
                if eb < 8:
                    tr.emit("dve", lambda h, b=b, dst=dst: h.tensor_copy(out=dst, in_=psum[b][:, :]),
                            reads=[("PS", b)], writes=[key])
                else:
                    tr.emit("act", lambda h, b=b, dst=dst: h.activation(out=dst, in_=psum[b][:, :], func=AF.Gelu),
                            reads=[("PS", b)], writes=[key])
            if 1 <= i <= 8:
                gmlp_head(i - 1)
            if i == 12:
                tr.barrier("S5")
                for g in range(8):
                    b = next_bank()
                    mm_group(b, [(psum[b][:, 0:128], ccsc[:, 0:128], wfb[:, g, :], True, True),
                                 (psum[b][:, 128:256], ccsc[:, 128:256], wfb[:, g, :], True, True)],
                             reads=[("S5", "ccsc"), ("S5", "wfb")])
                    tr.emit("dve", lambda h, b=b, g=g: h.tensor_copy(out=Mg[:, g, :], in_=psum[b][:, 0:256]),
                            reads=[("PS", b)], writes=[("S5", "Mg", g)])
            if i == 9:
                tr.barrier("R3")
                dma("sp", tab[0].rearrange("p a s k -> p (a s k)"), tabs_d[0], "d_tab0", writes=[("R3", "tab", 0)])
        dump("mixA0", mix[0])
        dump("mixA1", mix[1])

        tr.barrier("R0")
        tr.barrier("R4")
        dma("sp", tab[1].rearrange("p a s k -> p (a s k)"), tabs_d[1], "d_tab1", writes=[("R4", "tab", 1)])
        it = 0
        for tt in range(16):
            th, toff = tt // 8, (tt % 8) * 128
            for gp in range(4):
                b = next_bank()
                mms = []
                rd = []
                for gi in range(2):
                    g = 2 * gp + gi
                    rd.append(("S5", "Mg", g))
                    mms.append((psum[b][:, gi * 256:(gi + 1) * 256], mix[th][:, g, toff:toff + 128],
                                Mg[:, g, :], True, True))
                    rd.append(("R1" if th == 0 else "R2", "mix", g, (tt % 8) // 4))
                mm_group(b, mms, reads=rd)
                dst = Abuf[:, tt, 2 * gp:2 * gp + 2, :].rearrange("p g m -> p (g m)")
                if it % 2 == 0:
                    tr.emit("dve", lambda h, b=b, dst=dst: h.tensor_copy(out=dst, in_=psum[b][:, :]),
                            reads=[("PS", b)], writes=[("R0", "A", tt, gp)])
                else:
                    tr.emit("act", lambda h, b=b, dst=dst: h.activation(out=dst, in_=psum[b][:, :], func=AF.Copy),
                            reads=[("PS", b)], writes=[("R0", "A", tt, gp)])
                it += 1

        dump("A", Abuf)
        tr.barrier("S5")
        qs = [view(S5 + 8192 + i * 2048, 2048, F32) for i in range(2)]
        it = 0
        for kb in range(2):
            sl = kb
            reg = "R3" if sl == 0 else "R4"
            for g in range(8):
                bP = next_bank()
                bQ = next_bank()
                k = it % 2
                it += 1
                rdA = [(reg, "tab", sl)] + [("R0", "A", st, g // 2) for st in range(16)]
                mm_group(bP, [(psum[bP][:, :], Abuf[:, st, g, 0:128], tab[sl][:, 0, st, :], st == 0, st == 15)
                              for st in range(16)], reads=rdA)
                mm_group(bQ, [(psum[bQ][:, :], Abuf[:, st, g, 128:256], tab[sl][:, 1, st, :], st == 0, st == 15)
                              for st in range(16)], reads=rdA)
                tr.emit("act", lambda h, bQ=bQ, k=k: h.activation(out=qs[k], in_=psum[bQ][:, :], func=AF.Copy),
                        reads=[("PS", bQ)], writes=[("S5", "qs", k)])
                dst = mix[0][:, g, kb * 512:(kb + 1) * 512]
                tr.emit("dve", lambda h, bP=bP, k=k, dst=dst: h.tensor_tensor(
                    out=dst, in0=psum[bP][:, :], in1=qs[k], op=ALU.subtract),
                    reads=[("PS", bP), ("S5", "qs", k)], writes=[("R1", "mix", g, kb)])
                c0 = 1 if kb == 0 else 0
                n = 512 - c0
                first = 1024 - (kb * 512 + c0)
                fwd = mix[1][:, g, first:first + 1]
                rev = bass.AP(fwd.tensor, fwd.offset, [list(fwd.ap[0]), [-1, n]])
                wk = [("R2", "mix", g, 1)] if kb == 0 else [("R2", "mix", g, 0), ("R2", "mix", g, 1)]
                tr.emit("dve", lambda h, bP=bP, k=k, rev=rev, c0=c0: h.tensor_tensor(
                    out=rev, in0=psum[bP][:, c0:512], in1=qs[k][:, c0:512], op=ALU.add),
                    reads=[("PS", bP), ("S5", "qs", k)], writes=wk)
        bN = next_bank()
        mmsN = []
        for g in range(8):
            for st in range(16):
                mmsN.append((psum[bN][:, g:g + 1], Abuf[:, st, g, 0:128], ccsc[:, 256:257], st == 0, st == 15))
        mm_group(bN, mmsN, reads=[("S5", "ccsc")] + [("R0", "A", st, gp) for st in range(16) for gp in range(4)])
        tr.emit("dve", lambda h, bN=bN: h.tensor_copy(out=mix[1][:, 0:8, 0], in_=psum[bN][:, 0:8]),
                reads=[("PS", bN)], writes=[("R2", "mix", g, 0) for g in range(8)])

        dump("mixB2_0", mix[0])
        dump("mixB2_1", mix[1])
        tr.barrier("R4")
        w_limit[0] = len(wq)
        tr.barrier("R0")
        tr.barrier("S5")
        tr.region_prev["S5a"] = dict(tr.region_prev["S5"])
        xr = [view(S5 + i * 4096, 4096, F32) for i in range(2)]
        rl = [view(S5 + i * 4096, 2048, F32) for i in range(2)]
        rs_c = [view(S5 + 8192 + i * 2048, 2048, F32) for i in range(2)]
        sqc = [view(S5 + 12288 + i * 1024, 1024, BF16) for i in range(2)]
        h1a = view(R1, 32768, F32, "p (c t) -> p c t", t=1024)
        h1b = view(R3, 32768, F32, "p (c t) -> p c t", t=1024)
        r1 = view(R0, 32768, BF16, "p (c t) -> p c t", t=1024)
        hn1 = view(R0 + 32768, 32768, BF16, "p (c t) -> p c t", t=1024)

        def hv(t, dc, tsl):
            if t == 0:
                return hbuf[:, dc, tsl]
            return h1a[:, dc, tsl] if dc < 8 else h1b[:, dc - 8, tsl]

        def hk(t, dc, tb2):
            if t == 0:
                return ("R0", "h", dc, tb2)
            return ("R1" if dc < 8 else "R3", "h1", dc, tb2)

        def rv(t, fc, tsl):
            return (rbuf if t == 0 else r1)[:, fc, tsl]

        def rk(t, fc, tb2):
            return ("R1", "r", fc, tb2) if t == 0 else ("R0", "r1", fc, tb2)

        def hnv(t, dc, tsl):
            return (hn if t == 0 else hn1)[:, dc, tsl]

        def hnk(t, dc, tb2):
            return ("R3", "hn", dc, tb2) if t == 0 else ("R0", "hn1", dc, tb2)

        def rms_stats_gen(t, tb2, gcol, dst_fn, dst_keys_fn, pre_bank=None):
            tsl = slice(tb2 * 512, (tb2 + 1) * 512)
            if pre_bank is None:
                b = next_bank()
                reserved.add(b)
            else:
                b = pre_bank
            for dc in (range(16) if pre_bank is None else ()):
                k = sq_ctr[0] % 2
                sq_ctr[0] += 1
                tr.emit("act", lambda h, dc=dc, k=k: h.activation(out=sqc[k], in_=hv(t, dc, tsl), func=AF.Square),
                        reads=[hk(t, dc, tb2)], writes=[("S5", "sqc", k)])
                tr.emit("pe", lambda h, dc=dc, k=k, b=b: h.matmul(psum[b][:, :], ones, sqc[k], start=(dc == 0), stop=(dc == 15)),
                        reads=[("S5", "sqc", k), ("C", "ones")], writes=[("PS", b)])
                yield
            tr.emit("act", lambda h, b=b: h.activation(
                out=rs_c[tb2], in_=psum[b][:, :], func=AF.Sqrt, bias=epsb[:, 0:1], scale=1.0),
                reads=[("PS", b), ("C", "eps")], writes=[("S5", "rsc", tb2)])
            tr.emit("dve", lambda h: h.reciprocal(out=rs_c[tb2], in_=rs_c[tb2]),
                    reads=[("S5", "rsc", tb2)], writes=[("S5", "rsc", tb2)])
            reserved.discard(b)
            yield
            for dc in range(16):
                tr.emit("dve", lambda h, dc=dc: h.scalar_tensor_tensor(
                    out=dst_fn(dc, tsl), in0=hv(t, dc, tsl), scalar=cvec[:, gcol + dc:gcol + dc + 1],
                    in1=rs_c[tb2], op0=ALU.mult, op1=ALU.mult),
                    reads=[hk(t, dc, tb2), ("S5", "rsc", tb2), ("C", "cvec")],
                    writes=[dst_keys_fn(dc, tb2)])
                yield

        def e_half_gen(t, tb2, pre_bank=None):
            yield from rms_stats_gen(t, tb2, 32, lambda dc, tsl, t=t: hv(t, dc, tsl),
                                     lambda dc, tb2, t=t: hk(t, dc, tb2), pre_bank=pre_bank)
            t0 = t * 1024 + tb2 * 512
            for j in range(4):
                if t == 0:
                    src = hbuf[:, 4 * j:4 * j + 4, tb2 * 512:(tb2 + 1) * 512]
                else:
                    src = (h1a if j < 2 else h1b)[:, 4 * (j % 2):4 * (j % 2) + 4, tb2 * 512:(tb2 + 1) * 512]
                dma("sp", yT_v[:, 4 * j:4 * j + 4, t0:t0 + 512], src, f"d_out{j}",
                    reads=[hk(t, dc, tb2) for dc in range(4 * j, 4 * j + 4)])
                yield

        def e_half(t, tb2, pre_bank=None):
            for _ in e_half_gen(t, tb2, pre_bank=pre_bank):
                pass

        sq_ctr = [0]
        e0_gen = [None]
        r0_barrier_done = [False]

        def e0_steps(n):
            g = e0_gen[0]
            if g is None:
                return
            for _ in range(n):
                if next(g, "done") == "done":
                    e0_gen[0] = None
                    return

        xr_done = set()

        def issue_xr(t, db):
            if (t, db) in xr_done:
                return
            xr_done.add((t, db))
            k = db % 2
            dma("sp", xr[k], xT_v[:, db, t * 1024:(t + 1) * 1024], f"d_xr{k}", writes=[("S5a", "xr", k)])

        for t in range(2):
            mreg = "R1" if t == 0 else "R2"
            t0 = t * 1024
            if t == 0:
                tr.barrier("S5a")
            if t == 1:
                tr.barrier("R1")
                tr.barrier("R3")
            sbank = [next_bank(), next_bank()]
            reserved.update(sbank)
            pend_stats = []
            pend_hn = []

            def emit_stat(db, tb2, k):
                bb = sbank[tb2]
                tr.emit("pe", lambda h, k=k, bb=bb, db=db: h.matmul(psum[bb][:, :], ones, sqc[k], start=(db == 0), stop=(db == 15)),
                        reads=[("S5", "sqc", k), ("C", "ones")], writes=[("PS", bb)])

            def emit_hn(db, tb2):
                if t == 1 and not r0_barrier_done[0]:
                    e0_steps(10 ** 6)
                    tr.barrier("R0")
                    r0_barrier_done[0] = True
                tsl = slice(tb2 * 512, (tb2 + 1) * 512)
                tr.emit("act", lambda h, db=db, tsl=tsl, t=t: h.activation(
                    out=hnv(t, db, tsl), in_=hv(t, db, tsl), func=AF.Copy, scale=cvec[:, 16 + db:17 + db]),
                    reads=[hk(t, db, tb2), ("C", "cvec")], writes=[hnk(t, db, tb2)])

            for db in range(16):
                s = take_weight()
                k = db % 2
                issue_xr(t, db)
                for tb2 in range(2):
                    b = next_bank()
                    tsl = slice(tb2 * 512, (tb2 + 1) * 512)
                    mm_group(b, [(psum[b][:, :], wslot[s][:, ec, :], mix[t][:, ec, tsl], ec == 0, ec == 15)
                                 for ec in range(16)],
                             reads=[("R4", "w", s)] + [(mreg, "mix", ec, tb2) for ec in range(16)])
                    if pend_stats:
                        emit_stat(*pend_stats.pop(0))
                    if t == 1:
                        e0_steps(5)
                    tr.emit("dve", lambda h, b=b, db=db, tsl=tsl, k=k, t=t: h.tensor_tensor(
                        out=hv(t, db, tsl), in0=psum[b][:, :], in1=xr[k][:, tsl], op=ALU.add),
                        reads=[("PS", b), ("S5a", "xr", k)], writes=[hk(t, db, tb2)])
                    ks = sq_ctr[0] % 2
                    sq_ctr[0] += 1
                    tr.emit("act", lambda h, db=db, tsl=tsl, ks=ks, t=t: h.activation(
                        out=sqc[ks], in_=hv(t, db, tsl), func=AF.Square),
                        reads=[hk(t, db, tb2)], writes=[("S5", "sqc", ks)])
                    pend_stats.append((db, tb2, ks))
                    if t == 0:
                        emit_hn(db, tb2)
                    else:
                        pend_hn.append((db, tb2))
                if t == 1 and db >= 9:
                    for _ in range(5):
                        if pend_hn:
                            emit_hn(*pend_hn.pop(0))
            while pend_stats:
                emit_stat(*pend_stats.pop(0))
            if t == 1:
                e0_steps(10 ** 6)
            while pend_hn:
                emit_hn(*pend_hn.pop(0))
            for tb2 in range(2):
                tr.emit("dve", lambda h, tb2=tb2, sb=sbank[tb2]: h.tensor_scalar(
                    out=rs_c[tb2], in0=psum[sb][:, :], scalar1=EPS, scalar2=None, op0=ALU.add),
                    reads=[("PS", sbank[tb2])], writes=[("S5", "rsc", tb2)])
                tr.emit("dve", lambda h, tb2=tb2: h.reciprocal(out=rs_c[tb2], in_=rs_c[tb2]),
                        reads=[("S5", "rsc", tb2)], writes=[("S5", "rsc", tb2)])
            reserved.difference_update(sbank)
            if t == 0:
                dump("hC", hbuf)
            if t == 0:
                tr.barrier("R1")
            tr.barrier("S5a")
            it = 0
            for q in range(4):
                for fc in range(16):
                    s = take_weight()
                    for tb2 in range(2):
                        b = next_bank()
                        k = it % 2
                        it += 1
                        tsl = slice(tb2 * 512, (tb2 + 1) * 512)
                        mm_group(b, [(psum[b][:, :], wslot[s][:, dc, :], hnv(t, dc, tsl), dc == 0, dc == 15)
                                     for dc in range(16)],
                                 reads=[("R4", "w", s)] + [hnk(t, dc, tb2) for dc in range(16)])
                        tr.emit("act", lambda h, b=b, k=k: h.activation(out=rl[k], in_=psum[b][:, :], func=AF.Relu),
                                reads=[("PS", b)], writes=[("S5a", "rl", k)])
                        tr.emit("act", lambda h, k=k: h.activation(out=rl[k], in_=rl[k], func=AF.Square),
                                reads=[("S5a", "rl", k)], writes=[("S5a", "rl", k)])
                        tr.emit("dve", lambda h, k=k, fc=fc, tsl=tsl, t=t, tb2=tb2: h.tensor_tensor(
                            out=rv(t, fc, tsl), in0=rl[k], in1=rs_c[tb2], op=ALU.mult),
                            reads=[("S5a", "rl", k), ("S5", "rsc", tb2)], writes=[rk(t, fc, tb2)])
                def down_group(s, db, tb2):
                    b = next_bank()
                    tsl = slice(tb2 * 512, (tb2 + 1) * 512)
                    mm_group(b, [(psum[b][:, :], wslot[s][:, fc, :], rv(t, fc, tsl), fc == 0, fc == 15)
                                 for fc in range(16)],
                             reads=[("R4", "w", s)] + [rk(t, fc, tb2) for fc in range(16)])
                    tr.emit("dve", lambda h, b=b, db=db, tsl=tsl, t=t: h.tensor_tensor(
                        out=hv(t, db, tsl), in0=psum[b][:, :], in1=hv(t, db, tsl), op=ALU.add),
                        reads=[("PS", b), hk(t, db, tb2)], writes=[hk(t, db, tb2)])

                if t == 1 and q == 3:
                    for db in range(16):
                        s = take_weight()
                        down_group(s, db, 0)
                    fb = next_bank()
                    reserved.add(fb)
                    pend = None
                    tslf = slice(512, 1024)

                    def f_stat(db, ks):
                        tr.emit("pe", lambda h, ks=ks, db=db: h.matmul(psum[fb][:, :], ones, sqc[ks], start=(db == 0), stop=(db == 15)),
                                reads=[("S5", "sqc", ks), ("C", "ones")], writes=[("PS", fb)])

                    for db in range(16):
                        s = take_weight()
                        down_group(s, db, 1)
                        if pend is not None:
                            f_stat(*pend)
                            pend = None
                        if db == 3:
                            e_half(1, 0)
                        ks = sq_ctr[0] % 2
                        sq_ctr[0] += 1
                        tr.emit("act", lambda h, db=db, ks=ks: h.activation(out=sqc[ks], in_=hv(1, db, tslf), func=AF.Square),
                                reads=[hk(1, db, 1)], writes=[("S5", "sqc", ks)])
                        pend = (db, ks)
                    f_stat(*pend)
                    e_half(1, 1, pre_bank=fb)
                else:
                    for db in range(16):
                        s = take_weight()
                        for tb2 in range(2):
                            down_group(s, db, tb2)
            if t == 0:
                dump("hD", hbuf)
            if t == 0:
                tr.barrier("S5a")
                issue_xr(1, 0)
                issue_xr(1, 1)
                import itertools
                e0_gen[0] = itertools.chain(e_half_gen(0, 0), e_half_gen(0, 1))

        for j in range(4):
            tr.prog["sp"].append(("wait", f"d_out{j}", tr.count[f"d_out{j}"]))

        _check_no_deadlock(tr)
        semkeys = sorted(tr.count.keys())
        sems = {k: es.enter_context(nc.semaphore(k)) for k in semkeys}

        def replay(eng, h):
            for item in tr.prog[eng]:
                if item[0] == "wait":
                    h.wait_ge(sems[item[1]], item[2])
                else:
                    _, fn, sk, inc = item
                    ins = fn(h)
                    if sk is not None:
                        ins.then_inc(sems[sk], inc)

        with nc.Block() as block:
            @block.sync
            def _(h):
                replay("sp", h)

            @block.gpsimd
            def _(h):
                replay("pool", h)

            @block.tensor
            def _(h):
                replay("pe", h)

            @block.scalar
            def _(h):
                replay("act", h)

            @block.vector
            def _(h):
                replay("dve", h)
    nc._dbg_outs = dbg_outs
    return nc


def _tile_w(w, nblk):
    K, N = w.shape
    j = N // nblk
    t = w.reshape(K // 128, 128, nblk, j).transpose(2, 1, 0, 3)
    return np.ascontiguousarray(t).reshape(nblk, 128, (K // 128) * j)


def _const_tables():
    bf = ml_dtypes.bfloat16
    m = np.arange(128)
    ang = 2.0 * np.pi * ((m[:, None] * m[None, :]) % 128).astype(np.float64) / 128.0
    ccsc = np.zeros((128, 272), dtype=np.float64)
    ccsc[:, 0:128] = np.cos(ang) / 512.0
    ccsc[:, 128:256] = np.sin(ang) / 512.0
    ccsc[:, 256] = 1.0 - 2.0 * (m % 2)
    s = np.arange(S)
    k = np.arange(S // 2)
    angs = 2.0 * np.pi * ((s[:, None] * k[None, :]) % S).astype(np.float64) / S
    tabs = np.stack([np.cos(angs), np.sin(angs)], axis=0)
    tabs = tabs.reshape(2, 16, 128, 2, 512).transpose(3, 2, 0, 1, 4)
    tabs = np.ascontiguousarray(tabs).reshape(2, 128, 2 * 16 * 512)
    return ccsc.astype(np.float32).astype(bf), tabs.astype(np.float32).astype(bf)


_CACHE = {}


def _prep(x, norm_mix_g, w_in, fourier_w, gmlp_v_g, gmlp_ws, gmlp_b, w_out,
          norm_mlp_g, w_up, w_down, norm_final_g):
    x = np.asarray(x, dtype=np.float32)
    f = lambda a: np.asarray(a, dtype=np.float32)
    w_in, w_out, w_up, w_down = f(w_in), f(w_out), f(w_up), f(w_down)
    if "tables" not in _CACHE:
        _CACHE["tables"] = _const_tables()
    ccsc, tabs = _CACHE["tables"]
    wfm = _tile_w(w_in[:, :2048], 16)
    wv = _tile_w(w_in[:, 2048:], 2)
    wout = _tile_w(w_out, 16)
    wup = _tile_w(w_up, 64)
    wdn = np.ascontiguousarray(
        w_down.reshape(4, 16, 128, 16, 128).transpose(0, 3, 2, 1, 4)).reshape(64, 128, 2048)
    cvec = np.concatenate([f(norm_mix_g).reshape(16, 128).T, f(norm_mlp_g).reshape(16, 128).T,
                           f(norm_final_g).reshape(16, 128).T, f(gmlp_v_g).T], axis=1)
    cvec = np.ascontiguousarray(cvec, dtype=np.float32)
    gb = np.ascontiguousarray(f(gmlp_b))
    wsT = np.ascontiguousarray(f(gmlp_ws).transpose(2, 0, 1)).reshape(128, 1024)
    wf = np.ascontiguousarray(f(fourier_w).transpose(1, 0, 2)).reshape(128, 1024)
    shared = {"wfm": wfm, "wv": wv, "wout": wout, "wup": wup, "wdn": wdn, "cvec": cvec, "gb": gb,
              "wsT": wsT, "wf": wf, "ccsc": ccsc, "tabs": tabs}
    in_maps = []
    for b in range(NCORES):
        m = dict(shared)
        m["xT"] = np.ascontiguousarray(x[b].T)
        in_maps.append(m)
    return in_maps


def kernel(x, norm_mix_g, w_in, fourier_w, gmlp_v_g, gmlp_ws, gmlp_b, w_out,
           norm_mlp_g, w_up, w_down, norm_final_g):
    if "nc" not in _CACHE:
        _CACHE["nc"] = build_nc()
    nc = _CACHE["nc"]
    in_maps = _prep(x, norm_mix_g, w_in, fourier_w, gmlp_v_g, gmlp_ws, gmlp_b, w_out,
                    norm_mlp_g, w_up, w_down, norm_final_g)
    res = run_bass_kernel_spmd(nc, in_maps, core_ids=list(range(NCORES)))
    out = np.empty((NCORES, S, D), dtype=np.float32)
    for b in range(NCORES):
        out[b] = np.asarray(res.results[b]["yT"]).T
    return out
```

```python
import math
from contextlib import ExitStack

import ml_dtypes
import numpy as np

import concourse.bass as bass
import concourse.mybir as mybir
from concourse.bass_utils import run_bass_kernel_spmd

F32 = mybir.dt.float32
BF16 = mybir.dt.bfloat16
U8 = mybir.dt.uint8
AF = mybir.ActivationFunctionType
ALU = mybir.AluOpType
AX = mybir.AxisListType

D = 2048
S = 2048
NCH = 16
DFF = 8192
EPS = 1e-6
NCORES = 8
NSLOT = 8

R0 = 0
R1 = 65536
R2 = 98304
R3 = 131072
R4 = 163840
R5 = 196608
CVEC_OFF = R5
ONES_OFF = R5 + 224
ONEV_OFF = R5 + 480
RSTD_OFF = R5 + 512
S5 = R5 + 1024
ARENA = 212736
S5_SIZE = ARENA - S5


class Tracker:
    def __init__(self):
        self.prog = {}
        self.count = {}
        self.seen = {}
        self.state = {}
        self.region_prev = {}
        self.engsem = {}

    def add_engine(self, eng, semkey):
        self.prog[eng] = []
        self.seen[eng] = {}
        self.engsem[eng] = semkey
        self.count.setdefault(semkey, 0)

    def _st(self, key):
        if key not in self.state:
            self.state[key] = {"w": {}, "r": {}}
        return self.state[key]

    def barrier(self, region):
        prev = self.region_prev.setdefault(region, {})
        for key, st in self.state.items():
            if key[0] == region:
                for d in (st["w"], st["r"]):
                    for s, v in d.items():
                        if prev.get(s, 0) < v:
                            prev[s] = v

    def emit(self, eng, fn, reads=(), writes=(), dma_sem=None, inc=True):
        deps = {}

        def add(d):
            for s, v in d.items():
                if deps.get(s, 0) < v:
                    deps[s] = v

        own = self.engsem[eng]
        for k in reads:
            add(self._st(k)["w"])
        for k in writes:
            st = self._st(k)
            skip = own if eng == "pe" else None
            add({s: v for s, v in st["w"].items() if s != skip})
            add({s: v for s, v in st["r"].items() if s != skip})
            add({s: v for s, v in self.region_prev.get(k[0], {}).items() if s != skip})
        seen = self.seen[eng]
        for s, v in deps.items():
            if seen.get(s, 0) < v:
                self.prog[eng].append(("wait", s, v))
                seen[s] = v
        if dma_sem is not None:
            self.count.setdefault(dma_sem, 0)
            self.count[dma_sem] += 16
            tok = (dma_sem, self.count[dma_sem])
            self.prog[eng].append(("op", fn, dma_sem, 16))
        elif inc:
            self.count[own] += 1
            tok = (own, self.count[own])
            self.prog[eng].append(("op", fn, own, 1))
        else:
            tok = None
            self.prog[eng].append(("op", fn, None, 0))
        if tok is not None:
            for k in reads:
                d = self._st(k)["r"]
                if d.get(tok[0], 0) < tok[1]:
                    d[tok[0]] = tok[1]
            for k in writes:
                d = self._st(k)["w"]
                if d.get(tok[0], 0) < tok[1]:
                    d[tok[0]] = tok[1]
        return tok


def _check_no_deadlock(tr):
    pc = {e: 0 for e in tr.prog}
    sem = {}
    progress = True
    while progress:
        progress = False
        for e, items in tr.prog.items():
            while pc[e] < len(items):
                it = items[pc[e]]
                if it[0] == "wait":
                    if sem.get(it[1], 0) < it[2]:
                        break
                else:
                    if it[2] is not None:
                        sem[it[2]] = sem.get(it[2], 0) + it[3]
                pc[e] += 1
                progress = True
    stuck = {e: (pc[e], len(items)) for e, items in tr.prog.items() if pc[e] < len(items)}
    assert not stuck, f"deadlock in recorded program: {stuck}"
    for k, v in tr.count.items():
        assert sem.get(k, 0) == v, (k, sem.get(k, 0), v)


def build_nc(debug=False):
    nc = bass.Bass("TRN2", target_bir_lowering=False)
    dbg_outs = []
    xT = nc.dram_tensor("xT", [D, S], F32, kind="ExternalInput").ap()
    wfm = nc.dram_tensor("wfm", [16, 128, 2048], F32, kind="ExternalInput").ap()
    wv = nc.dram_tensor("wv", [2, 128, 8192], F32, kind="ExternalInput").ap()
    wout = nc.dram_tensor("wout", [16, 128, 2048], F32, kind="ExternalInput").ap()
    wup = nc.dram_tensor("wup", [64, 128, 2048], F32, kind="ExternalInput").ap()
    wdn = nc.dram_tensor("wdn", [64, 128, 2048], F32, kind="ExternalInput").ap()
    cvec_d = nc.dram_tensor("cvec", [128, 56], F32, kind="ExternalInput").ap()
    gb_d = nc.dram_tensor("gb", [8, 128], F32, kind="ExternalInput")
    wsT_d = nc.dram_tensor("wsT", [128, 1024], F32, kind="ExternalInput").ap()
    wf_d = nc.dram_tensor("wf", [128, 1024], F32, kind="ExternalInput").ap()
    ccsc_d = nc.dram_tensor("ccsc", [128, 272], BF16, kind="ExternalInput").ap()
    tabs_d = nc.dram_tensor("tabs", [2, 128, 16384], BF16, kind="ExternalInput").ap()
    yT = nc.dram_tensor("yT", [D, S], F32, kind="ExternalOutput").ap()

    xT_v = xT.rearrange("(c p) t -> p c t", p=128)
    yT_v = yT.rearrange("(c p) t -> p c t", p=128)

    tr = Tracker()
    for eng in ("pe", "act", "dve", "pool", "sp"):
        tr.add_engine(eng, eng)

    with ExitStack() as es:
        arena = es.enter_context(nc.sbuf_tensor("arena", [128, ARENA], U8))
        psum = [es.enter_context(nc.psum_tensor(f"ps{i}", [128, 512], F32)) for i in range(8)]

        def view(off, nbytes, dt, pat=None, **kw):
            v = arena[:, off:off + nbytes].bitcast(dt)
            if pat is not None:
                v = v.rearrange(pat, **kw)
            return v

        cvec = view(CVEC_OFF, 224, F32)
        ones = view(ONES_OFF, 256, BF16)
        epsb = view(ONEV_OFF, 4, F32)
        rstd_all = view(RSTD_OFF, 512, F32)
        xn = view(R0, 65536, BF16, "p (c t) -> p c t", t=S)
        Abuf = view(R0, 65536, BF16, "p (t g m) -> p t g m", g=8, m=256)
        hbuf = view(R0, 65536, F32, "p (c t) -> p c t", t=1024)
        mix = [view(R1, 32768, BF16, "p (c t) -> p c t", t=1024),
               view(R2, 32768, BF16, "p (c t) -> p c t", t=1024)]
        rbuf = view(R1, 32768, BF16, "p (c t) -> p c t", t=1024)
        xs = [view(R3 + i * 16384, 16384, F32, "p (c t) -> p c t", t=256) for i in range(2)]
        vtok = view(R3, 32768, BF16, "p (t e) -> p t e", e=1024)
        hn = view(R3, 32768, BF16, "p (c t) -> p c t", t=1024)
        tab = [view(R3, 32768, BF16, "p (a s k) -> p a s k", a=2, k=512),
               view(R4, 32768, BF16, "p (a s k) -> p a s k", a=2, k=512)]
        wslot = [view(R4 + i * 4096, 4096, BF16, "p (c j) -> p c j", j=128) for i in range(NSLOT)]
        wvslot = [view(R4 + i * 16384, 16384, BF16, "p (c j) -> p c j", j=512) for i in range(2)]

        bank_ctr = [0]
        reserved = set()

        def next_bank():
            while True:
                b = bank_ctr[0] % 8
                bank_ctr[0] += 1
                if b not in reserved:
                    return b

        def dma(eng, out, in_, sem, reads=(), writes=()):
            tr.emit(eng, lambda h, o=out, i=in_: h.dma_start(out=o, in_=i),
                    reads=reads, writes=writes, dma_sem=sem)

        def dump(name, src):
            if not debug:
                return
            shp = list(src.shape)
            dt = nc.dram_tensor("dbg_" + name, shp, src.dtype, kind="ExternalOutput").ap()
            dbg_outs.append("dbg_" + name)
            sk = "d_dbg_" + name
            tr.count[sk] = 16
            for s_, v_ in tr.count.items():
                if s_ != sk and tr.seen["sp"].get(s_, 0) < v_:
                    tr.prog["sp"].append(("wait", s_, v_))
                    tr.seen["sp"][s_] = v_
            tr.prog["sp"].append(("op", lambda h, o=dt, i=src: h.dma_start(out=o, in_=i), sk, 16))
            for e in tr.prog:
                tr.prog[e].append(("wait", sk, 16))
                tr.seen[e][sk] = 16

        def mm_group(bank, mms, reads, extra_writes=()):
            n = len(mms)

            def fn(h, mms=mms):
                last = None
                for (o, l, r, st, sp) in mms:
                    last = h.matmul(o, l, r, start=st, stop=sp)
                return last
            tr.emit("pe", fn, reads=reads, writes=[("PS", bank)] + list(extra_writes))

        wq = []
        wq.append(("v", wv[0]))
        wq.append(("v", wv[1]))
        EB_ORDER = list(range(8, 16)) + list(range(8))
        for eb in EB_ORDER:
            wq.append(("std", wfm[eb]))
        for t in range(2):
            for db in range(16):
                wq.append(("std", wout[db]))
            for q in range(4):
                for fc in range(16):
                    wq.append(("std", wup[q * 16 + fc]))
                for rep in range(2 if (t == 1 and q == 3) else 1):
                    for db in range(16):
                        wq.append(("std", wdn[q * 16 + db]))
        w_issued = [0]
        w_limit = [1]
        w_slot_of = {}
        slot_ctr = [0]

        slot_tile = [-1] * NSLOT
        w_extra_reads = []

        xslot = [view(R1 + i * 4096, 4096, BF16, "p (c j) -> p c j", j=128) for i in range(4)]
        XT0 = 2

        def wtile(sid):
            return wslot[sid] if sid < 100 else xslot[sid - 100]

        def wkey(sid):
            return ("R4", "w", sid) if sid < 100 else ("R1", "wx", sid - 100)

        def issue_x_tiles():
            for j in range(4):
                i = XT0 + j
                assert w_issued[0] == i
                dma("pool", xslot[j].rearrange("p c j -> p (c j)"), wq[i][1], f"d_wx{j}",
                    writes=[("R1", "wx", j)])
                w_slot_of[i] = 100 + j
                w_issued[0] += 1

        def issue_weights(upto, cur):
            while w_issued[0] <= min(upto, len(wq) - 1, w_limit[0]):
                i = w_issued[0]
                kind, src = wq[i]
                if kind == "std":
                    s = slot_ctr[0] % NSLOT
                    need = [s]
                    adv = 1
                else:
                    pad = (-slot_ctr[0]) % 4
                    s = (slot_ctr[0] + pad) % NSLOT
                    need = [s + j for j in range(4)]
                    adv = pad + 4
                if any(slot_tile[q] >= cur for q in need):
                    break
                slot_ctr[0] += adv
                for q in need:
                    slot_tile[q] = i
                if kind == "std":
                    dst = wslot[s].rearrange("p c j -> p (c j)")
                else:
                    dst = wvslot[s // 4].rearrange("p c j -> p (c j)")
                keys = [("R4", "w", q) for q in need]
                w_slot_of[i] = s
                dma("pool", dst, src, f"d_w{s}", reads=list(w_extra_reads), writes=keys)
                w_issued[0] += 1

        LOOK = 6
        w_next = [0]

        def take_weight():
            i = w_next[0]
            w_next[0] += 1
            issue_weights(i + LOOK, i)
            assert i in w_slot_of, i
            return w_slot_of[i]

        dma("sp", cvec, cvec_d, "d_c0", writes=[("C", "cvec")])
        tr.emit("dve", lambda h: h.memset(ones, 1.0 / D), writes=[("C", "ones")])
        tr.emit("dve", lambda h: h.memset(epsb, EPS), writes=[("C", "eps")])
        w_limit[0] = 0
        sv0 = take_weight()
        wvkeys = [[("R4", "w", sv0 + j) for j in range(4)], None]

        xs = [view(R1 + i * 16384, 16384, F32, "p (c t) -> p c t", t=256) for i in range(2)]
        rs_a = [view(S5 + i * 1024, 1024, F32) for i in range(2)]
        sq16 = view(S5 + 2048, 8192, BF16, "p (c t) -> p c t", t=256)
        sqv = [view(S5 + 10240 + i * 2048, 2048, F32) for i in range(2)]
        ss_all = view(S5 + 14336, 512, F32)
        it = 0

        def a0_sq(tb):
            sl = tb % 2
            tsl = slice(tb * 256, (tb + 1) * 256)
            dma("sp", xs[sl], xT_v[:, :, tsl], f"d_xs{sl}", reads=(wvkeys[0] if tb == 1 else []),
                writes=[("R1", "xs", sl)])
            tr.emit("act", lambda h, sl=sl: h.activation(out=sq16, in_=xs[sl], func=AF.Square),
                    reads=[("R1", "xs", sl)], writes=[("S5", "sq16")])

        def a0_rest(tb):
            sl = tb % 2
            tsl = slice(tb * 256, (tb + 1) * 256)
            b = next_bank()
            mm_group(b, [(psum[b][:, 0:256], ones, sq16[:, c, :], c == 0, c == 15) for c in range(16)],
                     reads=[("S5", "sq16"), ("C", "ones")])
            tr.emit("act", lambda h, b=b, sl=sl: h.activation(
                out=rs_a[sl], in_=psum[b][:, 0:256], func=AF.Sqrt, bias=epsb[:, 0:1], scale=1.0),
                reads=[("PS", b), ("C", "eps")], writes=[("S5", "rs", sl)])
            tr.emit("dve", lambda h, sl=sl: h.reciprocal(out=rs_a[sl], in_=rs_a[sl]),
                    reads=[("S5", "rs", sl)], writes=[("S5", "rs", sl)])
            for c in range(16):
                tr.emit("dve", lambda h, c=c, sl=sl, tsl=tsl: h.scalar_tensor_tensor(
                    out=xn[:, c, tsl], in0=xs[sl][:, c, :], scalar=cvec[:, c:c + 1], in1=rs_a[sl],
                    op0=ALU.mult, op1=ALU.mult),
                    reads=[("R1", "xs", sl), ("S5", "rs", sl), ("C", "cvec")],
                    writes=[("R0", "xn", tb, c)])

        def a2_block(tb):
            nonlocal_it = a2_it
            for cb in range(2):
                for tt in (2 * tb, 2 * tb + 1):
                    b = next_bank()
                    k = nonlocal_it[0] % 2
                    nonlocal_it[0] += 1
                    mm_group(b, [(psum[b][:, :], xn[:, c, tt * 128:(tt + 1) * 128], wvv[cb][:, c, :], c == 0, c == 15)
                                 for c in range(16)],
                             reads=wvkeys[cb] + [("R0", "xn", tb, c) for c in range(16)])
                    vblk = vtok[:, tt, cb * 512:(cb + 1) * 512]
                    tr.emit("act", lambda h, b=b, vblk=vblk: h.activation(out=vblk, in_=psum[b][:, :], func=AF.Gelu),
                            reads=[("PS", b)], writes=[("R3", "v", tt, cb)])
                    tr.emit("dve", lambda h, k=k, vblk=vblk: h.tensor_tensor(out=sqv[k], in0=vblk, in1=vblk, op=ALU.mult),
                            reads=[("R3", "v", tt, cb)], writes=[("S5", "sqv", k)])
                    c0 = tt * 8 + cb * 4
                    tr.emit("dve", lambda h, k=k, c0=c0: h.tensor_reduce(
                        out=ss_all[:, c0:c0 + 4], in_=sqv[k].rearrange("p (a d) -> p a d", d=128),
                        axis=AX.X, op=ALU.add),
                        reads=[("S5", "sqv", k)], writes=[("S5", "ss_all", c0)])

        a2_it = [0]
        a0_sq(0)
        w_limit[0] = 1
        w_extra_reads.append(("R1", "xs", 0))
        sv1 = take_weight()
        w_extra_reads.clear()
        wvkeys[1] = [("R4", "w", sv1 + j) for j in range(4)]
        wvv = [wvslot[sv0 // 4], wvslot[sv1 // 4]]
        a0_rest(0)
        a0_sq(1)
        for tb in range(8):
            if tb + 1 < 8:
                a0_rest(tb + 1)
                if tb + 1 == 7:
                    tr.barrier("R1")
                    issue_x_tiles()
            if tb + 2 < 8:
                a0_sq(tb + 2)
            a2_block(tb)
        tr.emit("act", lambda h: h.activation(out=rstd_all, in_=ss_all, func=AF.Sqrt, bias=epsb[:, 0:1],
                                              scale=1.0 / 128.0),
                reads=[("S5", "ss_all", tt * 8 + cb * 4) for tt in range(16) for cb in range(2)] + [("C", "eps")],
                writes=[("C", "rstd_all")])
        tr.emit("dve", lambda h: h.reciprocal(out=rstd_all, in_=rstd_all),
                reads=[("C", "rstd_all")], writes=[("C", "rstd_all")])
        w_limit[0] = 17
        dump("xn", xn)
        dump("vtok", vtok)
        dump("rstd", rstd_all)

        tr.barrier("R1")
        tr.barrier("S5")
        bbc = view(S5, 4096, F32, "p (h q) -> p h q", q=128)
        wsT = view(S5 + 4096, 2048, BF16, "p (h q) -> p h q", q=128)
        tb3 = [view(S5 + 6144, 2048, F32)] * 2
        wfb = view(S5 + 8192, 2048, BF16, "p (g d) -> p g d", d=128)
        ccsc = view(S5 + 14336, 544, BF16)
        Mg = view(S5, 4096, BF16, "p (g m) -> p g m", m=256)
        dma("sp", ccsc, ccsc_d, "d_c3", writes=[("S5", "ccsc")])
        dma("pool", wfb.rearrange("p g d -> p (g d)"), wf_d, "d_c4", writes=[("S5", "wfb")])
        NWS = 16
        wsS = [view(S5 + 10240 + i * 256, 256, BF16) for i in range(NWS)]
        dma("sp", bbc, bass.AP(gb_d, 0, [[0, 128], [128, 8], [1, 128]]), "d_c1", writes=[("S5", "bbc")])
        dma("pool", wsT.rearrange("p h q -> p (h q)"), wsT_d, "d_c2", writes=[("S5", "wsT")])
        b3_it = [0]
        b3_iw = [0]

        def gmlp_scale(hd):
            for tt in range(16):
                col = tt * 8 + hd
                tr.emit("act", lambda h, tt=tt, hd=hd, col=col: h.activation(
                    out=wsS[tt], in_=wsT[:, hd, :], func=AF.Copy, scale=rstd_all[:, col:col + 1]),
                    reads=[("S5", "wsT"), ("C", "rstd_all")], writes=[("S5", "wsS", tt)])

        def gmlp_head(hd):
            for tb in range(4):
                b = next_bank()
                k = 0
                mms = []
                rd = []
                for j in range(4):
                    tt = tb * 4 + j
                    kw = tt
                    mms.append((psum[b][:, j * 128:(j + 1) * 128], vtok[:, tt, hd * 128:(hd + 1) * 128],
                                wsS[kw], True, True))
                    rd += [("S5", "wsS", kw), ("R3", "v", tt, hd // 4)]
                mm_group(b, mms, reads=rd)
                th, off = tb // 2, (tb % 2) * 512
                ublk = mix[th][:, 8 + hd, off:off + 512]
                key = ("R1" if th == 0 else "R2", "mix", 8 + hd, tb % 2)
                tr.emit("dve", lambda h, b=b, k=k, hd=hd: h.scalar_tensor_tensor(
                    out=tb3[k].rearrange("p (a q) -> p a q", q=128),
                    in0=psum[b][:, :].rearrange("p (a q) -> p a q", q=128),
                    scalar=cvec[:, 48 + hd:49 + hd],
                    in1=bbc[:, hd, :].unsqueeze(1).to_broadcast([128, 4, 128]),
                    op0=ALU.mult, op1=ALU.add),
                    reads=[("PS", b), ("S5", "bbc"), ("C", "cvec")], writes=[("S5", "t3", k)])
                tr.emit("dve", lambda h, k=k, ublk=ublk: h.tensor_tensor(
                    out=ublk, in0=tb3[k], in1=ublk, op=ALU.mult),
                    reads=[("S5", "t3", k), key], writes=[key])

        for i, eb in enumerate(EB_ORDER):
            s = take_weight()
            if i == 8:
                tr.barrier("R1")
            if 1 <= i <= 8:
                gmlp_scale(i - 1)
            for tb in range(4):
                b = next_bank()
                tsl = slice(tb * 512, (tb + 1) * 512)
                mm_group(b, [(psum[b][:, :], wtile(s)[:, c, :], xn[:, c, tsl], c == 0, c == 15)
                             for c in range(16)],
                         reads=[wkey(s)] + [("R0", "xn", 2 * tb + u, c) for u in range(2) for c in range(16)])
                th, off = tb // 2, (tb % 2) * 512
                dst = mix[th][:, eb, off:off + 512]
                key = ("R1" if th == 0 else "R2", "mix", eb, tb % 2)
                if eb < 8:
                    tr.emit("dve", lambda h, b=b, dst=dst: h.tensor_copy(out=dst, in_=psum[b][:, :]),
                            reads=[("PS", b)], writes=[key])
                else:
                    tr.emit("act", lambda h, b=b, dst=dst: h.activation(out=dst, in_=psum[b][:, :], func=AF.Gelu),
                            reads=[("PS", b)], writes=[key])
            if 1 <= i <= 8:
                gmlp_head(i - 1)
            if i == 12:
                tr.barrier("S5")
                for g in range(8):
                    b = next_bank()
                    mm_group(b, [(psum[b][:, 0:128], ccsc[:, 0:128], wfb[:, g, :], True, True),
                                 (psum[b][:, 128:256], ccsc[:, 128:256], wfb[:, g, :], True, True)],
                             reads=[("S5", "ccsc"), ("S5", "wfb")])
                    tr.emit("dve", lambda h, b=b, g=g: h.tensor_copy(out=Mg[:, g, :], in_=psum[b][:, 0:256]),
                            reads=[("PS", b)], writes=[("S5", "Mg", g)])
            if i == 9:
                tr.barrier("R3")
                dma("sp", tab[0].rearrange("p a s k -> p (a s k)"), tabs_d[0], "d_tab0", writes=[("R3", "tab", 0)])
        dump("mixA0", mix[0])
        dump("mixA1", mix[1])

        tr.barrier("R0")
        tr.barrier("R4")
        dma("sp", tab[1].rearrange("p a s k -> p (a s k)"), tabs_d[1], "d_tab1", writes=[("R4", "tab", 1)])
        it = 0
        for tt in range(16):
            th, toff = tt // 8, (tt % 8) * 128
            for gp in range(4):
                b = next_bank()
                mms = []
                rd = []
                for gi in range(2):
                    g = 2 * gp + gi
                    rd.append(("S5", "Mg", g))
                    mms.append((psum[b][:, gi * 256:(gi + 1) * 256], mix[th][:, g, toff:toff + 128],
                                Mg[:, g, :], True, True))
                    rd.append(("R1" if th == 0 else "R2", "mix", g, (tt % 8) // 4))
                mm_group(b, mms, reads=rd)
                dst = Abuf[:, tt, 2 * gp:2 * gp + 2, :].rearrange("p g m -> p (g m)")
                if it % 2 == 0:
                    tr.emit("dve", lambda h, b=b, dst=dst: h.tensor_copy(out=dst, in_=psum[b][:, :]),
                            reads=[("PS", b)], writes=[("R0", "A", tt, gp)])
                else:
                    tr.emit("act", lambda h, b=b, dst=dst: h.activation(out=dst, in_=psum[b][:, :], func=AF.Copy),
                            reads=[("PS", b)], writes=[("R0", "A", tt, gp)])
                it += 1

        dump("A", Abuf)
        tr.barrier("S5")
        qs = [view(S5 + 8192 + i * 2048, 2048, F32) for i in range(2)]
        it = 0
        for kb in range(2):
            sl = kb
            reg = "R3" if sl == 0 else "R4"
            for g in range(8):
                bP = next_bank()
                bQ = next_bank()
                k = it % 2
                it += 1
                rdA = [(reg, "tab", sl)] + [("R0", "A", st, g // 2) for st in range(16)]
                mm_group(bP, [(psum[bP][:, :], Abuf[:, st, g, 0:128], tab[sl][:, 0, st, :], st == 0, st == 15)
                              for st in range(16)], reads=rdA)
                mm_group(bQ, [(psum[bQ][:, :], Abuf[:, st, g, 128:256], tab[sl][:, 1, st, :], st == 0, st == 15)
                              for st in range(16)], reads=rdA)
                tr.emit("act", lambda h, bQ=bQ, k=k: h.activation(out=qs[k], in_=psum[bQ][:, :], func=AF.Copy),
                        reads=[("PS", bQ)], writes=[("S5", "qs", k)])
                dst = mix[0][:, g, kb * 512:(kb + 1) * 512]
                tr.emit("dve", lambda h, bP=bP, k=k, dst=dst: h.tensor_tensor(
                    out=dst, in0=psum[bP][:, :], in1=qs[k], op=ALU.subtract),
                    reads=[("PS", bP), ("S5", "qs", k)], writes=[("R1", "mix", g, kb)])
                c0 = 1 if kb == 0 else 0
                n = 512 - c0
                first = 1024 - (kb * 512 + c0)
                fwd = mix[1][:, g, first:first + 1]
                rev = bass.AP(fwd.tensor, fwd.offset, [list(fwd.ap[0]), [-1, n]])
                wk = [("R2", "mix", g, 1)] if kb == 0 else [("R2", "mix", g, 0), ("R2", "mix", g, 1)]
                tr.emit("dve", lambda h, bP=bP, k=k, rev=rev, c0=c0: h.tensor_tensor(
                    out=rev, in0=psum[bP][:, c0:512], in1=qs[k][:, c0:512], op=ALU.add),
                    reads=[("PS", bP), ("S5", "qs", k)], writes=wk)
        bN = next_bank()
        mmsN = []
        for g in range(8):
            for st in range(16):
                mmsN.append((psum[bN][:, g:g + 1], Abuf[:, st, g, 0:128], ccsc[:, 256:257], st == 0, st == 15))
        mm_group(bN, mmsN, reads=[("S5", "ccsc")] + [("R0", "A", st, gp) for st in range(16) for gp in range(4)])
        tr.emit("dve", lambda h, bN=bN: h.tensor_copy(out=mix[1][:, 0:8, 0], in_=psum[bN][:, 0:8]),
                reads=[("PS", bN)], writes=[("R2", "mix", g, 0) for g in range(8)])

        dump("mixB2_0", mix[0])
        dump("mixB2_1", mix[1])
        tr.barrier("R4")
        w_limit[0] = len(wq)
        tr.barrier("R0")
        tr.barrier("S5")
        tr.region_prev["S5a"] = dict(tr.region_prev["S5"])
        xr = [view(S5 + i * 4096, 4096, F32) for i in range(2)]
        rl = [view(S5 + i * 4096, 2048, F32) for i in range(2)]
        rs_c = [view(S5 + 8192 + i * 2048, 2048, F32) for i in range(2)]
        sqc = [view(S5 + 12288 + i * 1024, 1024, BF16) for i in range(2)]
        h1a = view(R1, 32768, F32, "p (c t) -> p c t", t=1024)
        h1b = view(R3, 32768, F32, "p (c t) -> p c t", t=1024)
        r1 = view(R0, 32768, BF16, "p (c t) -> p c t", t=1024)
        hn1 = view(R0 + 32768, 32768, BF16, "p (c t) -> p c t", t=1024)

        def hv(t, dc, tsl):
            if t == 0:
                return hbuf[:, dc, tsl]
            return h1a[:, dc, tsl] if dc < 8 else h1b[:, dc - 8, tsl]

        def hk(t, dc, tb2):
            if t == 0:
                return ("R0", "h", dc, tb2)
            return ("R1" if dc < 8 else "R3", "h1", dc, tb2)

        def rv(t, fc, tsl):
            return (rbuf if t == 0 else r1)[:, fc, tsl]

        def rk(t, fc, tb2):
            return ("R1", "r", fc, tb2) if t == 0 else ("R0", "r1", fc, tb2)

        def hnv(t, dc, tsl):
            return (hn if t == 0 else hn1)[:, dc, tsl]

        def hnk(t, dc, tb2):
            return ("R3", "hn", dc, tb2) if t == 0 else ("R0", "hn1", dc, tb2)

        def rms_stats_gen(t, tb2, gcol, dst_fn, dst_keys_fn, pre_bank=None):
            tsl = slice(tb2 * 512, (tb2 + 1) * 512)
            if pre_bank is None:
                b = next_bank()
                reserved.add(b)
            else:
                b = pre_bank
            for dc in (range(16) if pre_bank is None else ()):
                k = sq_ctr[0] % 2
                sq_ctr[0] += 1
                tr.emit("act", lambda h, dc=dc, k=k: h.activation(out=sqc[k], in_=hv(t, dc, tsl), func=AF.Square),
                        reads=[hk(t, dc, tb2)], writes=[("S5", "sqc", k)])
                tr.emit("pe", lambda h, dc=dc, k=k, b=b: h.matmul(psum[b][:, :], ones, sqc[k], start=(dc == 0), stop=(dc == 15)),
                        reads=[("S5", "sqc", k), ("C", "ones")], writes=[("PS", b)])
                yield
            tr.emit("act", lambda h, b=b: h.activation(
                out=rs_c[tb2], in_=psum[b][:, :], func=AF.Sqrt, bias=epsb[:, 0:1], scale=1.0),
                reads=[("PS", b), ("C", "eps")], writes=[("S5", "rsc", tb2)])
            tr.emit("dve", lambda h: h.reciprocal(out=rs_c[tb2], in_=rs_c[tb2]),
                    reads=[("S5", "rsc", tb2)], writes=[("S5", "rsc", tb2)])
            reserved.discard(b)
            yield
            for dc in range(16):
                tr.emit("dve", lambda h, dc=dc: h.scalar_tensor_tensor(
                    out=dst_fn(dc, tsl), in0=hv(t, dc, tsl), scalar=cvec[:, gcol + dc:gcol + dc + 1],
                    in1=rs_c[tb2], op0=ALU.mult, op1=ALU.mult),
                    reads=[hk(t, dc, tb2), ("S5", "rsc", tb2), ("C", "cvec")],
                    writes=[dst_keys_fn(dc, tb2)])
                yield

        def e_half_gen(t, tb2, pre_bank=None):
            yield from rms_stats_gen(t, tb2, 32, lambda dc, tsl, t=t: hv(t, dc, tsl),
                                     lambda dc, tb2, t=t: hk(t, dc, tb2), pre_bank=pre_bank)
            t0 = t * 1024 + tb2 * 512
            for j in range(4):
                if t == 0:
                    src = hbuf[:, 4 * j:4 * j + 4, tb2 * 512:(tb2 + 1) * 512]
                else:
                    src = (h1a if j < 2 else h1b)[:, 4 * (j % 2):4 * (j % 2) + 4, tb2 * 512:(tb2 + 1) * 512]
                dma("sp", yT_v[:, 4 * j:4 * j + 4, t0:t0 + 512], src, f"d_out{j}",
                    reads=[hk(t, dc, tb2) for dc in range(4 * j, 4 * j + 4)])
                yield

        def e_half(t, tb2, pre_bank=None):
            for _ in e_half_gen(t, tb2, pre_bank=pre_bank):
                pass

        sq_ctr = [0]
        e0_gen = [None]
        r0_barrier_done = [False]

        def e0_steps(n):
            g = e0_gen[0]
            if g is None:
                return
            for _ in range(n):
                if next(g, "done") == "done":
                    e0_gen[0] = None
                    return

        xr_done = set()

        def issue_xr(t, db):
            if (t, db) in xr_done:
                return
            xr_done.add((t, db))
            k = db % 2
            dma("sp", xr[k], xT_v[:, db, t * 1024:(t + 1) * 1024], f"d_xr{k}", writes=[("S5a", "xr", k)])

        for t in range(2):
            mreg = "R1" if t == 0 else "R2"
            t0 = t * 1024
            if t == 0:
                tr.barrier("S5a")
            if t == 1:
                tr.barrier("R1")
                tr.barrier("R3")
            sbank = [next_bank(), next_bank()]
            reserved.update(sbank)
            pend_stats = []
            pend_hn = []

            def emit_stat(db, tb2, k):
                bb = sbank[tb2]
                tr.emit("pe", lambda h, k=k, bb=bb, db=db: h.matmul(psum[bb][:, :], ones, sqc[k], start=(db == 0), stop=(db == 15)),
                        reads=[("S5", "sqc", k), ("C", "ones")], writes=[("PS", bb)])

            def emit_hn(db, tb2):
                if t == 1 and not r0_barrier_done[0]:
                    e0_steps(10 ** 6)
                    tr.barrier("R0")
                    r0_barrier_done[0] = True
                tsl = slice(tb2 * 512, (tb2 + 1) * 512)
                tr.emit("act", lambda h, db=db, tsl=tsl, t=t: h.activation(
                    out=hnv(t, db, tsl), in_=hv(t, db, tsl), func=AF.Copy, scale=cvec[:, 16 + db:17 + db]),
                    reads=[hk(t, db, tb2), ("C", "cvec")], writes=[hnk(t, db, tb2)])

            for db in range(16):
                s = take_weight()
                k = db % 2
                issue_xr(t, db)
                for tb2 in range(2):
                    b = next_bank()
                    tsl = slice(tb2 * 512, (tb2 + 1) * 512)
                    mm_group(b, [(psum[b][:, :], wslot[s][:, ec, :], mix[t][:, ec, tsl], ec == 0, ec == 15)
                                 for ec in range(16)],
                             reads=[("R4", "w", s)] + [(mreg, "mix", ec, tb2) for ec in range(16)])
                    if pend_stats:
                        emit_stat(*pend_stats.pop(0))
                    if t == 1:
                        e0_steps(5)
                    tr.emit("dve", lambda h, b=b, db=db, tsl=tsl, k=k, t=t: h.tensor_tensor(
                        out=hv(t, db, tsl), in0=psum[b][:, :], in1=xr[k][:, tsl], op=ALU.add),
                        reads=[("PS", b), ("S5a", "xr", k)], writes=[hk(t, db, tb2)])
                    ks = sq_ctr[0] % 2
                    sq_ctr[0] += 1
                    tr.emit("act", lambda h, db=db, tsl=tsl, ks=ks, t=t: h.activation(
                        out=sqc[ks], in_=hv(t, db, tsl), func=AF.Square),
                        reads=[hk(t, db, tb2)], writes=[("S5", "sqc", ks)])
                    pend_stats.append((db, tb2, ks))
                    if t == 0:
                        emit_hn(db, tb2)
                    else:
                        pend_hn.append((db, tb2))
                if t == 1 and db >= 9:
                    for _ in range(5):
                        if pend_hn:
                            emit_hn(*pend_hn.pop(0))
            while pend_stats:
                emit_stat(*pend_stats.pop(0))
            if t == 1:
                e0_steps(10 ** 6)
            while pend_hn:
                emit_hn(*pend_hn.pop(0))
            for tb2 in range(2):
                tr.emit("dve", lambda h, tb2=tb2, sb=sbank[tb2]: h.tensor_scalar(
                    out=rs_c[tb2], in0=psum[sb][:, :], scalar1=EPS, scalar2=None, op0=ALU.add),
                    reads=[("PS", sbank[tb2])], writes=[("S5", "rsc", tb2)])
                tr.emit("dve", lambda h, tb2=tb2: h.reciprocal(out=rs_c[tb2], in_=rs_c[tb2]),
                        reads=[("S5", "rsc", tb2)], writes=[("S5", "rsc", tb2)])
            reserved.difference_update(sbank)
            if t == 0:
                dump("hC", hbuf)
            if t == 0:
                tr.barrier("R1")
            tr.barrier("S5a")
            it = 0
            for q in range(4):
                for fc in range(16):
                    s = take_weight()
                    for tb2 in range(2):
                        b = next_bank()
                        k = it % 2
                        it += 1
                        tsl = slice(tb2 * 512, (tb2 + 1) * 512)
                        mm_group(b, [(psum[b][:, :], wslot[s][:, dc, :], hnv(t, dc, tsl), dc == 0, dc == 15)
                                     for dc in range(16)],
                                 reads=[("R4", "w", s)] + [hnk(t, dc, tb2) for dc in range(16)])
                        tr.emit("act", lambda h, b=b, k=k: h.activation(out=rl[k], in_=psum[b][:, :], func=AF.Relu),
                                reads=[("PS", b)], writes=[("S5a", "rl", k)])
                        tr.emit("act", lambda h, k=k: h.activation(out=rl[k], in_=rl[k], func=AF.Square),
                                reads=[("S5a", "rl", k)], writes=[("S5a", "rl", k)])
                        tr.emit("dve", lambda h, k=k, fc=fc, tsl=tsl, t=t, tb2=tb2: h.tensor_tensor(
                            out=rv(t, fc, tsl), in0=rl[k], in1=rs_c[tb2], op=ALU.mult),
                            reads=[("S5a", "rl", k), ("S5", "rsc", tb2)], writes=[rk(t, fc, tb2)])
                def down_group(s, db, tb2):
                    b = next_bank()
                    tsl = slice(tb2 * 512, (tb2 + 1) * 512)
                    mm_group(b, [(psum[b][:, :], wslot[s][:, fc, :], rv(t, fc, tsl), fc == 0, fc == 15)
                                 for fc in range(16)],
                             reads=[("R4", "w", s)] + [rk(t, fc, tb2) for fc in range(16)])
                    tr.emit("dve", lambda h, b=b, db=db, tsl=tsl, t=t: h.tensor_tensor(
                        out=hv(t, db, tsl), in0=psum[b][:, :], in1=hv(t, db, tsl), op=ALU.add),
                        reads=[("PS", b), hk(t, db, tb2)], writes=[hk(t, db, tb2)])

                if t == 1 and q == 3:
                    def last_pass(tb2, hook=None):
                        fbk = next_bank()
                        reserved.add(fbk)
                        pend = None
                        tslf = slice(tb2 * 512, (tb2 + 1) * 512)

                        def f_stat(db, ks):
                            tr.emit("pe", lambda h, ks=ks, db=db, fbk=fbk: h.matmul(
                                psum[fbk][:, :], ones, sqc[ks], start=(db == 0), stop=(db == 15)),
                                reads=[("S5", "sqc", ks), ("C", "ones")], writes=[("PS", fbk)])

                        for db in range(16):
                            s = take_weight()
                            down_group(s, db, tb2)
                            if pend is not None:
                                f_stat(*pend)
                                pend = None
                            if hook is not None and db == 3:
                                hook()
                            ks = sq_ctr[0] % 2
                            sq_ctr[0] += 1
                            tr.emit("act", lambda h, db=db, ks=ks, tslf=tslf: h.activation(
                                out=sqc[ks], in_=hv(1, db, tslf), func=AF.Square),
                                reads=[hk(1, db, tb2)], writes=[("S5", "sqc", ks)])
                            pend = (db, ks)
                        f_stat(*pend)
                        return fbk

                    fb0 = last_pass(0)
                    fb1 = last_pass(1, hook=lambda: e_half(1, 0, pre_bank=fb0))
                    e_half(1, 1, pre_bank=fb1)
                else:
                    for db in range(16):
                        s = take_weight()
                        for tb2 in range(2):
                            down_group(s, db, tb2)
            if t == 0:
                dump("hD", hbuf)
            if t == 0:
                tr.barrier("S5a")
                issue_xr(1, 0)
                issue_xr(1, 1)
                import itertools
                e0_gen[0] = itertools.chain(e_half_gen(0, 0), e_half_gen(0, 1))

        for j in range(4):
            tr.prog["sp"].append(("wait", f"d_out{j}", tr.count[f"d_out{j}"]))

        _check_no_deadlock(tr)
        semkeys = sorted(tr.count.keys())
        sems = {k: es.enter_context(nc.semaphore(k)) for k in semkeys}

        def replay(eng, h):
            for item in tr.prog[eng]:
                if item[0] == "wait":
                    h.wait_ge(sems[item[1]], item[2])
                else:
                    _, fn, sk, inc = item
                    ins = fn(h)
                    if sk is not None:
                        ins.then_inc(sems[sk], inc)

        with nc.Block() as block:
            @block.sync
            def _(h):
                replay("sp", h)

            @block.gpsimd
            def _(h):
                replay("pool", h)

            @block.tensor
            def _(h):
                replay("pe", h)

            @block.scalar
            def _(h):
                replay("act", h)

            @block.vector
            def _(h):
                replay("dve", h)
    nc._dbg_outs = dbg_outs
    return nc


def _tile_w(w, nblk):
    K, N = w.shape
    j = N // nblk
    t = w.reshape(K // 128, 128, nblk, j).transpose(2, 1, 0, 3)
    return np.ascontiguousarray(t).reshape(nblk, 128, (K // 128) * j)


def _const_tables():
    bf = ml_dtypes.bfloat16
    m = np.arange(128)
    ang = 2.0 * np.pi * ((m[:, None] * m[None, :]) % 128).astype(np.float64) / 128.0
    ccsc = np.zeros((128, 272), dtype=np.float64)
    ccsc[:, 0:128] = np.cos(ang) / 512.0
    ccsc[:, 128:256] = np.sin(ang) / 512.0
    ccsc[:, 256] = 1.0 - 2.0 * (m % 2)
    s = np.arange(S)
    k = np.arange(S // 2)
    angs = 2.0 * np.pi * ((s[:, None] * k[None, :]) % S).astype(np.float64) / S
    tabs = np.stack([np.cos(angs), np.sin(angs)], axis=0)
    tabs = tabs.reshape(2, 16, 128, 2, 512).transpose(3, 2, 0, 1, 4)
    tabs = np.ascontiguousarray(tabs).reshape(2, 128, 2 * 16 * 512)
    return ccsc.astype(np.float32).astype(bf), tabs.astype(np.float32).astype(bf)


_CACHE = {}


def _prep(x, norm_mix_g, w_in, fourier_w, gmlp_v_g, gmlp_ws, gmlp_b, w_out,
          norm_mlp_g, w_up, w_down, norm_final_g):
    x = np.asarray(x, dtype=np.float32)
    f = lambda a: np.asarray(a, dtype=np.float32)
    w_in, w_out, w_up, w_down = f(w_in), f(w_out), f(w_up), f(w_down)
    if "tables" not in _CACHE:
        _CACHE["tables"] = _const_tables()
    ccsc, tabs = _CACHE["tables"]
    wfm = _tile_w(w_in[:, :2048], 16)
    wv = _tile_w(w_in[:, 2048:], 2)
    wout = _tile_w(w_out, 16)
    wup = _tile_w(w_up, 64)
    wdn = np.ascontiguousarray(
        w_down.reshape(4, 16, 128, 16, 128).transpose(0, 3, 2, 1, 4)).reshape(64, 128, 2048)
    cvec = np.concatenate([f(norm_mix_g).reshape(16, 128).T, f(norm_mlp_g).reshape(16, 128).T,
                           f(norm_final_g).reshape(16, 128).T, f(gmlp_v_g).T], axis=1)
    cvec = np.ascontiguousarray(cvec, dtype=np.float32)
    gb = np.ascontiguousarray(f(gmlp_b))
    wsT = np.ascontiguousarray(f(gmlp_ws).transpose(2, 0, 1)).reshape(128, 1024)
    wf = np.ascontiguousarray(f(fourier_w).transpose(1, 0, 2)).reshape(128, 1024)
    shared = {"wfm": wfm, "wv": wv, "wout": wout, "wup": wup, "wdn": wdn, "cvec": cvec, "gb": gb,
              "wsT": wsT, "wf": wf, "ccsc": ccsc, "tabs": tabs}
    in_maps = []
    for b in range(NCORES):
        m = dict(shared)
        m["xT"] = np.ascontiguousarray(x[b].T)
        in_maps.append(m)
    return in_maps


def kernel(x, norm_mix_g, w_in, fourier_w, gmlp_v_g, gmlp_ws, gmlp_b, w_out,
           norm_mlp_g, w_up, w_down, norm_final_g):
    if "nc" not in _CACHE:
        _CACHE["nc"] = build_nc()
    nc = _CACHE["nc"]
    in_maps = _prep(x, norm_mix_g, w_in, fourier_w, gmlp_v_g, gmlp_ws, gmlp_b, w_out,
                    norm_mlp_g, w_up, w_down, norm_final_g)
    res = run_bass_kernel_spmd(nc, in_maps, core_ids=list(range(NCORES)))
    out = np.empty((NCORES, S, D), dtype=np.float32)
    for b in range(NCORES):
        out[b] = np.asarray(res.results[b]["yT"]).T
    return out
```

```python
import math
from contextlib import ExitStack

import ml_dtypes
import numpy as np

import concourse.bass as bass
import concourse.mybir as mybir
from concourse.bass_utils import run_bass_kernel_spmd

F32 = mybir.dt.float32
BF16 = mybir.dt.bfloat16
U8 = mybir.dt.uint8
AF = mybir.ActivationFunctionType
ALU = mybir.AluOpType
AX = mybir.AxisListType

D = 2048
S = 2048
NCH = 16
DFF = 8192
EPS = 1e-6
NCORES = 8
NSLOT = 8

R0 = 0
R1 = 65536
R2 = 98304
R3 = 131072
R4 = 163840
R5 = 196608
CVEC_OFF = R5
ONES_OFF = R5 + 224
ONEV_OFF = R5 + 480
RSTD_OFF = R5 + 512
S5 = R5 + 1024
ARENA = 212736
S5_SIZE = ARENA - S5


class Tracker:
    def __init__(self):
        self.prog = {}
        self.count = {}
        self.seen = {}
        self.state = {}
        self.region_prev = {}
        self.engsem = {}

    def add_engine(self, eng, semkey):
        self.prog[eng] = []
        self.seen[eng] = {}
        self.engsem[eng] = semkey
        self.count.setdefault(semkey, 0)

    def _st(self, key):
        if key not in self.state:
            self.state[key] = {"w": {}, "r": {}}
        return self.state[key]

    def barrier(self, region):
        prev = self.region_prev.setdefault(region, {})
        for key, st in self.state.items():
            if key[0] == region:
                for d in (st["w"], st["r"]):
                    for s, v in d.items():
                        if prev.get(s, 0) < v:
                            prev[s] = v

    def emit(self, eng, fn, reads=(), writes=(), dma_sem=None, inc=True):
        deps = {}

        def add(d):
            for s, v in d.items():
                if deps.get(s, 0) < v:
                    deps[s] = v

        own = self.engsem[eng]
        for k in reads:
            add(self._st(k)["w"])
        for k in writes:
            st = self._st(k)
            skip = own if eng == "pe" else None
            add({s: v for s, v in st["w"].items() if s != skip})
            add({s: v for s, v in st["r"].items() if s != skip})
            add({s: v for s, v in self.region_prev.get(k[0], {}).items() if s != skip})
        seen = self.seen[eng]
        for s, v in deps.items():
            if seen.get(s, 0) < v:
                self.prog[eng].append(("wait", s, v))
                seen[s] = v
        if dma_sem is not None:
            self.count.setdefault(dma_sem, 0)
            self.count[dma_sem] += 16
            tok = (dma_sem, self.count[dma_sem])
            self.prog[eng].append(("op", fn, dma_sem, 16))
        elif inc:
            self.count[own] += 1
            tok = (own, self.count[own])
            self.prog[eng].append(("op", fn, own, 1))
        else:
            tok = None
            self.prog[eng].append(("op", fn, None, 0))
        if tok is not None:
            for k in reads:
                d = self._st(k)["r"]
                if d.get(tok[0], 0) < tok[1]:
                    d[tok[0]] = tok[1]
            for k in writes:
                d = self._st(k)["w"]
                if d.get(tok[0], 0) < tok[1]:
                    d[tok[0]] = tok[1]
        return tok


def _check_no_deadlock(tr):
    pc = {e: 0 for e in tr.prog}
    sem = {}
    progress = True
    while progress:
        progress = False
        for e, items in tr.prog.items():
            while pc[e] < len(items):
                it = items[pc[e]]
                if it[0] == "wait":
                    if sem.get(it[1], 0) < it[2]:
                        break
                else:
                    if it[2] is not None:
                        sem[it[2]] = sem.get(it[2], 0) + it[3]
                pc[e] += 1
                progress = True
    stuck = {e: (pc[e], len(items)) for e, items in tr.prog.items() if pc[e] < len(items)}
    assert not stuck, f"deadlock in recorded program: {stuck}"
    for k, v in tr.count.items():
        assert sem.get(k, 0) == v, (k, sem.get(k, 0), v)


def build_nc(debug=False):
    nc = bass.Bass("TRN2", target_bir_lowering=False)
    dbg_outs = []
    xT = nc.dram_tensor("xT", [D, S], F32, kind="ExternalInput").ap()
    wfm = nc.dram_tensor("wfm", [16, 128, 2048], F32, kind="ExternalInput").ap()
    wv = nc.dram_tensor("wv", [2, 128, 8192], F32, kind="ExternalInput").ap()
    wout = nc.dram_tensor("wout", [16, 128, 2048], F32, kind="ExternalInput").ap()
    wup = nc.dram_tensor("wup", [64, 128, 2048], F32, kind="ExternalInput").ap()
    wdn = nc.dram_tensor("wdn", [64, 128, 2048], F32, kind="ExternalInput").ap()
    cvec_d = nc.dram_tensor("cvec", [128, 56], F32, kind="ExternalInput").ap()
    gb_d = nc.dram_tensor("gb", [8, 128], F32, kind="ExternalInput")
    wsT_d = nc.dram_tensor("wsT", [128, 1024], F32, kind="ExternalInput").ap()
    wf_d = nc.dram_tensor("wf", [128, 1024], F32, kind="ExternalInput").ap()
    ccsc_d = nc.dram_tensor("ccsc", [128, 272], BF16, kind="ExternalInput").ap()
    tabs_d = nc.dram_tensor("tabs", [2, 128, 16384], BF16, kind="ExternalInput").ap()
    yT = nc.dram_tensor("yT", [D, S], F32, kind="ExternalOutput").ap()

    xT_v = xT.rearrange("(c p) t -> p c t", p=128)
    yT_v = yT.rearrange("(c p) t -> p c t", p=128)

    tr = Tracker()
    for eng in ("pe", "act", "dve", "pool", "sp"):
        tr.add_engine(eng, eng)

    with ExitStack() as es:
        arena = es.enter_context(nc.sbuf_tensor("arena", [128, ARENA], U8))
        psum = [es.enter_context(nc.psum_tensor(f"ps{i}", [128, 512], F32)) for i in range(8)]

        def view(off, nbytes, dt, pat=None, **kw):
            v = arena[:, off:off + nbytes].bitcast(dt)
            if pat is not None:
                v = v.rearrange(pat, **kw)
            return v

        cvec = view(CVEC_OFF, 224, F32)
        ones = view(ONES_OFF, 256, BF16)
        epsb = view(ONEV_OFF, 4, F32)
        rstd_all = view(RSTD_OFF, 512, F32)
        xn = view(R0, 65536, BF16, "p (c t) -> p c t", t=S)
        Abuf = view(R0, 65536, BF16, "p (t g m) -> p t g m", g=8, m=256)
        hbuf = view(R0, 65536, F32, "p (c t) -> p c t", t=1024)
        mix = [view(R1, 32768, BF16, "p (c t) -> p c t", t=1024),
               view(R2, 32768, BF16, "p (c t) -> p c t", t=1024)]
        rbuf = view(R1, 32768, BF16, "p (c t) -> p c t", t=1024)
        xs = [view(R3 + i * 16384, 16384, F32, "p (c t) -> p c t", t=256) for i in range(2)]
        vtok = view(R3, 32768, BF16, "p (t e) -> p t e", e=1024)
        hn = view(R3, 32768, BF16, "p (c t) -> p c t", t=1024)
        tab = [view(R3, 32768, BF16, "p (a s k) -> p a s k", a=2, k=512),
               view(R4, 32768, BF16, "p (a s k) -> p a s k", a=2, k=512)]
        wslot = [view(R4 + i * 4096, 4096, BF16, "p (c j) -> p c j", j=128) for i in range(NSLOT)]
        wvslot = [view(R4 + i * 16384, 16384, BF16, "p (c j) -> p c j", j=512) for i in range(2)]

        bank_ctr = [0]
        reserved = set()

        def next_bank():
            while True:
                b = bank_ctr[0] % 8
                bank_ctr[0] += 1
                if b not in reserved:
                    return b

        def dma(eng, out, in_, sem, reads=(), writes=()):
            tr.emit(eng, lambda h, o=out, i=in_: h.dma_start(out=o, in_=i),
                    reads=reads, writes=writes, dma_sem=sem)

        def dump(name, src):
            if not debug:
                return
            shp = list(src.shape)
            dt = nc.dram_tensor("dbg_" + name, shp, src.dtype, kind="ExternalOutput").ap()
            dbg_outs.append("dbg_" + name)
            sk = "d_dbg_" + name
            tr.count[sk] = 16
            for s_, v_ in tr.count.items():
                if s_ != sk and tr.seen["sp"].get(s_, 0) < v_:
                    tr.prog["sp"].append(("wait", s_, v_))
                    tr.seen["sp"][s_] = v_
            tr.prog["sp"].append(("op", lambda h, o=dt, i=src: h.dma_start(out=o, in_=i), sk, 16))
            for e in tr.prog:
                tr.prog[e].append(("wait", sk, 16))
                tr.seen[e][sk] = 16

        def mm_group(bank, mms, reads, extra_writes=()):
            n = len(mms)

            def fn(h, mms=mms):
                last = None
                for (o, l, r, st, sp) in mms:
                    last = h.matmul(o, l, r, start=st, stop=sp)
                return last
            tr.emit("pe", fn, reads=reads, writes=[("PS", bank)] + list(extra_writes))

        wq = []
        wq.append(("v", wv[0]))
        wq.append(("v", wv[1]))
        EB_ORDER = list(range(8, 16)) + list(range(8))
        for eb in EB_ORDER:
            wq.append(("std", wfm[eb]))
        for t in range(2):
            for db in range(16):
                wq.append(("std", wout[db]))
            for q in range(4):
                for fc in range(16):
                    wq.append(("std", wup[q * 16 + fc]))
                for rep in range(2 if (t == 1 and q == 3) else 1):
                    for db in range(16):
                        wq.append(("std", wdn[q * 16 + db]))
        w_issued = [0]
        w_limit = [1]
        w_slot_of = {}
        slot_ctr = [0]

        slot_tile = [-1] * NSLOT
        w_extra_reads = []

        xslot = [view(R1 + i * 4096, 4096, BF16, "p (c j) -> p c j", j=128) for i in range(4)]
        XT0 = 2

        def wtile(sid):
            return wslot[sid] if sid < 100 else xslot[sid - 100]

        def wkey(sid):
            return ("R4", "w", sid) if sid < 100 else ("R1", "wx", sid - 100)

        def issue_x_tiles():
            for j in range(4):
                i = XT0 + j
                assert w_issued[0] == i
                dma("pool", xslot[j].rearrange("p c j -> p (c j)"), wq[i][1], f"d_wx{j}",
                    writes=[("R1", "wx", j)])
                w_slot_of[i] = 100 + j
                w_issued[0] += 1

        def issue_weights(upto, cur):
            while w_issued[0] <= min(upto, len(wq) - 1, w_limit[0]):
                i = w_issued[0]
                kind, src = wq[i]
                if kind == "std":
                    s = slot_ctr[0] % NSLOT
                    need = [s]
                    adv = 1
                else:
                    pad = (-slot_ctr[0]) % 4
                    s = (slot_ctr[0] + pad) % NSLOT
                    need = [s + j for j in range(4)]
                    adv = pad + 4
                if any(slot_tile[q] >= cur for q in need):
                    break
                slot_ctr[0] += adv
                for q in need:
                    slot_tile[q] = i
                if kind == "std":
                    dst = wslot[s].rearrange("p c j -> p (c j)")
                else:
                    dst = wvslot[s // 4].rearrange("p c j -> p (c j)")
                keys = [("R4", "w", q) for q in need]
                w_slot_of[i] = s
                dma("pool", dst, src, f"d_w{s}", reads=list(w_extra_reads), writes=keys)
                w_issued[0] += 1

        LOOK = 6
        w_next = [0]

        def take_weight():
            i = w_next[0]
            w_next[0] += 1
            issue_weights(i + LOOK, i)
            assert i in w_slot_of, i
            return w_slot_of[i]

        dma("sp", cvec, cvec_d, "d_c0", writes=[("C", "cvec")])
        tr.emit("dve", lambda h: h.memset(ones, 1.0 / D), writes=[("C", "ones")])
        tr.emit("dve", lambda h: h.memset(epsb, EPS), writes=[("C", "eps")])
        w_limit[0] = 0
        sv0 = take_weight()
        wvkeys = [[("R4", "w", sv0 + j) for j in range(4)], None]

        xs = [view(R1 + i * 16384, 16384, F32, "p (c t) -> p c t", t=256) for i in range(2)]
        rs_a = [view(S5 + i * 1024, 1024, F32) for i in range(2)]
        sq16 = view(S5 + 2048, 8192, BF16, "p (c t) -> p c t", t=256)
        sqv = [view(S5 + 10240 + i * 2048, 2048, F32) for i in range(2)]
        ss_all = view(S5 + 14336, 512, F32)
        it = 0

        def a0_sq(tb):
            sl = tb % 2
            tsl = slice(tb * 256, (tb + 1) * 256)
            dma("sp", xs[sl], xT_v[:, :, tsl], f"d_xs{sl}", reads=(wvkeys[0] if tb == 1 else []),
                writes=[("R1", "xs", sl)])
            tr.emit("act", lambda h, sl=sl: h.activation(out=sq16, in_=xs[sl], func=AF.Square),
                    reads=[("R1", "xs", sl)], writes=[("S5", "sq16")])

        def a0_rest(tb):
            sl = tb % 2
            tsl = slice(tb * 256, (tb + 1) * 256)
            b = next_bank()
            mm_group(b, [(psum[b][:, 0:256], ones, sq16[:, c, :], c == 0, c == 15) for c in range(16)],
                     reads=[("S5", "sq16"), ("C", "ones")])
            tr.emit("act", lambda h, b=b, sl=sl: h.activation(
                out=rs_a[sl], in_=psum[b][:, 0:256], func=AF.Sqrt, bias=epsb[:, 0:1], scale=1.0),
                reads=[("PS", b), ("C", "eps")], writes=[("S5", "rs", sl)])
            tr.emit("dve", lambda h, sl=sl: h.reciprocal(out=rs_a[sl], in_=rs_a[sl]),
                    reads=[("S5", "rs", sl)], writes=[("S5", "rs", sl)])
            for c in range(16):
                tr.emit("dve", lambda h, c=c, sl=sl, tsl=tsl: h.scalar_tensor_tensor(
                    out=xn[:, c, tsl], in0=xs[sl][:, c, :], scalar=cvec[:, c:c + 1], in1=rs_a[sl],
                    op0=ALU.mult, op1=ALU.mult),
                    reads=[("R1", "xs", sl), ("S5", "rs", sl), ("C", "cvec")],
                    writes=[("R0", "xn", tb, c)])

        def a2_block(tb):
            nonlocal_it = a2_it
            for cb in range(2):
                for tt in (2 * tb, 2 * tb + 1):
                    b = next_bank()
                    k = nonlocal_it[0] % 2
                    nonlocal_it[0] += 1
                    mm_group(b, [(psum[b][:, :], xn[:, c, tt * 128:(tt + 1) * 128], wvv[cb][:, c, :], c == 0, c == 15)
                                 for c in range(16)],
                             reads=wvkeys[cb] + [("R0", "xn", tb, c) for c in range(16)])
                    vblk = vtok[:, tt, cb * 512:(cb + 1) * 512]
                    tr.emit("act", lambda h, b=b, vblk=vblk: h.activation(out=vblk, in_=psum[b][:, :], func=AF.Gelu),
                            reads=[("PS", b)], writes=[("R3", "v", tt, cb)])
                    tr.emit("dve", lambda h, k=k, vblk=vblk: h.tensor_tensor(out=sqv[k], in0=vblk, in1=vblk, op=ALU.mult),
                            reads=[("R3", "v", tt, cb)], writes=[("S5", "sqv", k)])
                    c0 = tt * 8 + cb * 4
                    tr.emit("dve", lambda h, k=k, c0=c0: h.tensor_reduce(
                        out=ss_all[:, c0:c0 + 4], in_=sqv[k].rearrange("p (a d) -> p a d", d=128),
                        axis=AX.X, op=ALU.add),
                        reads=[("S5", "sqv", k)], writes=[("S5", "ss_all", c0)])

        a2_it = [0]
        a0_sq(0)
        w_limit[0] = 1
        w_extra_reads.append(("R1", "xs", 0))
        sv1 = take_weight()
        w_extra_reads.clear()
        wvkeys[1] = [("R4", "w", sv1 + j) for j in range(4)]
        wvv = [wvslot[sv0 // 4], wvslot[sv1 // 4]]
        a0_rest(0)
        a0_sq(1)
        for tb in range(8):
            if tb + 1 < 8:
                a0_rest(tb + 1)
                if tb + 1 == 7:
                    tr.barrier("R1")
                    issue_x_tiles()
            if tb + 2 < 8:
                a0_sq(tb + 2)
            a2_block(tb)
        tr.emit("act", lambda h: h.activation(out=rstd_all, in_=ss_all, func=AF.Sqrt, bias=epsb[:, 0:1],
                                              scale=1.0 / 128.0),
                reads=[("S5", "ss_all", tt * 8 + cb * 4) for tt in range(16) for cb in range(2)] + [("C", "eps")],
                writes=[("C", "rstd_all")])
        tr.emit("dve", lambda h: h.reciprocal(out=rstd_all, in_=rstd_all),
                reads=[("C", "rstd_all")], writes=[("C", "rstd_all")])
        w_limit[0] = 17
        dump("xn", xn)
        dump("vtok", vtok)
        dump("rstd", rstd_all)

        tr.barrier("R1")
        tr.barrier("S5")
        bbc = view(S5, 4096, F32, "p (h q) -> p h q", q=128)
        wsT = view(S5 + 4096, 2048, BF16, "p (h q) -> p h q", q=128)
        tb3 = [view(S5 + 6144, 2048, F32)] * 2
        wfb = view(S5 + 8192, 2048, BF16, "p (g d) -> p g d", d=128)
        ccsc = view(S5 + 14336, 544, BF16)
        Mg = view(S5, 4096, BF16, "p (g m) -> p g m", m=256)
        dma("sp", ccsc, ccsc_d, "d_c3", writes=[("S5", "ccsc")])
        dma("pool", wfb.rearrange("p g d -> p (g d)"), wf_d, "d_c4", writes=[("S5", "wfb")])
        NWS = 16
        wsS = [view(S5 + 10240 + i * 256, 256, BF16) for i in range(NWS)]
        dma("sp", bbc, bass.AP(gb_d, 0, [[0, 128], [128, 8], [1, 128]]), "d_c1", writes=[("S5", "bbc")])
        dma("pool", wsT.rearrange("p h q -> p (h q)"), wsT_d, "d_c2", writes=[("S5", "wsT")])
        b3_it = [0]
        b3_iw = [0]

        def gmlp_scale(hd):
            for tt in range(16):
                col = tt * 8 + hd
                tr.emit("act", lambda h, tt=tt, hd=hd, col=col: h.activation(
                    out=wsS[tt], in_=wsT[:, hd, :], func=AF.Copy, scale=rstd_all[:, col:col + 1]),
                    reads=[("S5", "wsT"), ("C", "rstd_all")], writes=[("S5", "wsS", tt)])

        def gmlp_head(hd):
            for tb in range(4):
                b = next_bank()
                k = 0
                mms = []
                rd = []
                for j in range(4):
                    tt = tb * 4 + j
                    kw = tt
                    mms.append((psum[b][:, j * 128:(j + 1) * 128], vtok[:, tt, hd * 128:(hd + 1) * 128],
                                wsS[kw], True, True))
                    rd += [("S5", "wsS", kw), ("R3", "v", tt, hd // 4)]
                mm_group(b, mms, reads=rd)
                th, off = tb // 2, (tb % 2) * 512
                ublk = mix[th][:, 8 + hd, off:off + 512]
                key = ("R1" if th == 0 else "R2", "mix", 8 + hd, tb % 2)
                tr.emit("dve", lambda h, b=b, k=k, hd=hd: h.scalar_tensor_tensor(
                    out=tb3[k].rearrange("p (a q) -> p a q", q=128),
                    in0=psum[b][:, :].rearrange("p (a q) -> p a q", q=128),
                    scalar=cvec[:, 48 + hd:49 + hd],
                    in1=bbc[:, hd, :].unsqueeze(1).to_broadcast([128, 4, 128]),
                    op0=ALU.mult, op1=ALU.add),
                    reads=[("PS", b), ("S5", "bbc"), ("C", "cvec")], writes=[("S5", "t3", k)])
                tr.emit("dve", lambda h, k=k, ublk=ublk: h.tensor_tensor(
                    out=ublk, in0=tb3[k], in1=ublk, op=ALU.mult),
                    reads=[("S5", "t3", k), key], writes=[key])

        for i, eb in enumerate(EB_ORDER):
            s = take_weight()
            if i == 8:
                tr.barrier("R1")
            if 1 <= i <= 8:
                gmlp_scale(i - 1)
            for tb in range(4):
                b = next_bank()
                tsl = slice(tb * 512, (tb + 1) * 512)
                mm_group(b, [(psum[b][:, :], wtile(s)[:, c, :], xn[:, c, tsl], c == 0, c == 15)
                             for c in range(16)],
                         reads=[wkey(s)] + [("R0", "xn", 2 * tb + u, c) for u in range(2) for c in range(16)])
                th, off = tb // 2, (tb % 2) * 512
                dst = mix[th][:, eb, off:off + 512]
                key = ("R1" if th == 0 else "R2", "mix", eb, tb % 2)
                if eb < 8:
                    tr.emit("dve", lambda h, b=b, dst=dst: h.tensor_copy(out=dst, in_=psum[b][:, :]),
                            reads=[("PS", b)], writes=[key])
                else:
                    tr.emit("act", lambda h, b=b, dst=dst: h.activation(out=dst, in_=psum[b][:, :], func=AF.Gelu),
                            reads=[("PS", b)], writes=[key])
            if 1 <= i <= 8:
                gmlp_head(i - 1)
            if i == 12:
                tr.barrier("S5")
                for g in range(8):
                    b = next_bank()
                    mm_group(b, [(psum[b][:, 0:128], ccsc[:, 0:128], wfb[:, g, :], True, True),
                                 (psum[b][:, 128:256], ccsc[:, 128:256], wfb[:, g, :], True, True)],
                             reads=[("S5", "ccsc"), ("S5", "wfb")])
                    tr.emit("dve", lambda h, b=b, g=g: h.tensor_copy(out=Mg[:, g, :], in_=psum[b][:, 0:256]),
                            reads=[("PS", b)], writes=[("S5", "Mg", g)])
            if i == 9:
                tr.barrier("R3")
                dma("sp", tab[0].rearrange("p a s k -> p (a s k)"), tabs_d[0], "d_tab0", writes=[("R3", "tab", 0)])
        dump("mixA0", mix[0])
        dump("mixA1", mix[1])

        tr.barrier("R0")
        tr.barrier("R4")
        dma("sp", tab[1].rearrange("p a s k -> p (a s k)"), tabs_d[1], "d_tab1", writes=[("R4", "tab", 1)])
        it = 0
        for tt in range(16):
            th, toff = tt // 8, (tt % 8) * 128
            for gp in range(4):
                b = next_bank()
                mms = []
                rd = []
                for gi in range(2):
                    g = 2 * gp + gi
                    rd.append(("S5", "Mg", g))
                    mms.append((psum[b][:, gi * 256:(gi + 1) * 256], mix[th][:, g, toff:toff + 128],
                                Mg[:, g, :], True, True))
                    rd.append(("R1" if th == 0 else "R2", "mix", g, (tt % 8) // 4))
                mm_group(b, mms, reads=rd)
                dst = Abuf[:, tt, 2 * gp:2 * gp + 2, :].rearrange("p g m -> p (g m)")
                if it % 2 == 0:
                    tr.emit("dve", lambda h, b=b, dst=dst: h.tensor_copy(out=dst, in_=psum[b][:, :]),
                            reads=[("PS", b)], writes=[("R0", "A", tt, gp)])
                else:
                    tr.emit("act", lambda h, b=b, dst=dst: h.activation(out=dst, in_=psum[b][:, :], func=AF.Copy),
                            reads=[("PS", b)], writes=[("R0", "A", tt, gp)])
                it += 1

        dump("A", Abuf)
        tr.barrier("S5")
        qs = [view(S5 + 8192 + i * 2048, 2048, F32) for i in range(2)]
        it = 0
        for kb in range(2):
            sl = kb
            reg = "R3" if sl == 0 else "R4"
            for g in range(8):
                bP = next_bank()
                bQ = next_bank()
                k = it % 2
                it += 1
                rdA = [(reg, "tab", sl)] + [("R0", "A", st, g // 2) for st in range(16)]
                mm_group(bP, [(psum[bP][:, :], Abuf[:, st, g, 0:128], tab[sl][:, 0, st, :], st == 0, st == 15)
                              for st in range(16)], reads=rdA)
                mm_group(bQ, [(psum[bQ][:, :], Abuf[:, st, g, 128:256], tab[sl][:, 1, st, :], st == 0, st == 15)
                              for st in range(16)], reads=rdA)
                tr.emit("act", lambda h, bQ=bQ, k=k: h.activation(out=qs[k], in_=psum[bQ][:, :], func=AF.Copy),
                        reads=[("PS", bQ)], writes=[("S5", "qs", k)])
                dst = mix[0][:, g, kb * 512:(kb + 1) * 512]
                tr.emit("dve", lambda h, bP=bP, k=k, dst=dst: h.tensor_tensor(
                    out=dst, in0=psum[bP][:, :], in1=qs[k], op=ALU.subtract),
                    reads=[("PS", bP), ("S5", "qs", k)], writes=[("R1", "mix", g, kb)])
                c0 = 1 if kb == 0 else 0
                n = 512 - c0
                first = 1024 - (kb * 512 + c0)
                fwd = mix[1][:, g, first:first + 1]
                rev = bass.AP(fwd.tensor, fwd.offset, [list(fwd.ap[0]), [-1, n]])
                wk = [("R2", "mix", g, 1)] if kb == 0 else [("R2", "mix", g, 0), ("R2", "mix", g, 1)]
                tr.emit("dve", lambda h, bP=bP, k=k, rev=rev, c0=c0: h.tensor_tensor(
                    out=rev, in0=psum[bP][:, c0:512], in1=qs[k][:, c0:512], op=ALU.add),
                    reads=[("PS", bP), ("S5", "qs", k)], writes=wk)
        bN = next_bank()
        mmsN = []
        for g in range(8):
            for st in range(16):
                mmsN.append((psum[bN][:, g:g + 1], Abuf[:, st, g, 0:128], ccsc[:, 256:257], st == 0, st == 15))
        mm_group(bN, mmsN, reads=[("S5", "ccsc")] + [("R0", "A", st, gp) for st in range(16) for gp in range(4)])
        tr.emit("dve", lambda h, bN=bN: h.tensor_copy(out=mix[1][:, 0:8, 0], in_=psum[bN][:, 0:8]),
                reads=[("PS", bN)], writes=[("R2", "mix", g, 0) for g in range(8)])

        dump("mixB2_0", mix[0])
        dump("mixB2_1", mix[1])
        tr.barrier("R4")
        w_limit[0] = len(wq)
        tr.barrier("R0")
        tr.barrier("S5")
        tr.region_prev["S5a"] = dict(tr.region_prev["S5"])
        xr = [view(S5 + i * 4096, 4096, F32) for i in range(2)]
        rl = [view(S5 + i * 4096, 2048, F32) for i in range(2)]
        rs_c = [view(S5 + 8192 + i * 2048, 2048, F32) for i in range(2)]
        sqc = [view(S5 + 12288 + i * 1024, 1024, BF16) for i in range(2)]
        h1a = view(R1, 32768, F32, "p (c t) -> p c t", t=1024)
        h1b = view(R3, 32768, F32, "p (c t) -> p c t", t=1024)
        r1 = view(R0, 32768, BF16, "p (c t) -> p c t", t=1024)
        hn1 = view(R0 + 32768, 32768, BF16, "p (c t) -> p c t", t=1024)

        def hv(t, dc, tsl):
            if t == 0:
                return hbuf[:, dc, tsl]
            return h1a[:, dc, tsl] if dc < 8 else h1b[:, dc - 8, tsl]

        def hk(t, dc, tb2):
            if t == 0:
                return ("R0", "h", dc, tb2)
            return ("R1" if dc < 8 else "R3", "h1", dc, tb2)

        def rv(t, fc, tsl):
            return (rbuf if t == 0 else r1)[:, fc, tsl]

        def rk(t, fc, tb2):
            return ("R1", "r", fc, tb2) if t == 0 else ("R0", "r1", fc, tb2)

        def hnv(t, dc, tsl):
            return (hn if t == 0 else hn1)[:, dc, tsl]

        def hnk(t, dc, tb2):
            return ("R3", "hn", dc, tb2) if t == 0 else ("R0", "hn1", dc, tb2)

        def rms_stats_gen(t, tb2, gcol, dst_fn, dst_keys_fn, pre_bank=None):
            tsl = slice(tb2 * 512, (tb2 + 1) * 512)
            if pre_bank is None:
                b = next_bank()
                reserved.add(b)
            else:
                b = pre_bank
            for dc in (range(16) if pre_bank is None else ()):
                k = sq_ctr[0] % 2
                sq_ctr[0] += 1
                tr.emit("act", lambda h, dc=dc, k=k: h.activation(out=sqc[k], in_=hv(t, dc, tsl), func=AF.Square),
                        reads=[hk(t, dc, tb2)], writes=[("S5", "sqc", k)])
                tr.emit("pe", lambda h, dc=dc, k=k, b=b: h.matmul(psum[b][:, :], ones, sqc[k], start=(dc == 0), stop=(dc == 15)),
                        reads=[("S5", "sqc", k), ("C", "ones")], writes=[("PS", b)])
                yield
            tr.emit("act", lambda h, b=b: h.activation(
                out=rs_c[tb2], in_=psum[b][:, :], func=AF.Sqrt, bias=epsb[:, 0:1], scale=1.0),
                reads=[("PS", b), ("C", "eps")], writes=[("S5", "rsc", tb2)])
            tr.emit("dve", lambda h: h.reciprocal(out=rs_c[tb2], in_=rs_c[tb2]),
                    reads=[("S5", "rsc", tb2)], writes=[("S5", "rsc", tb2)])
            reserved.discard(b)
            yield
            for dc in range(16):
                tr.emit("dve", lambda h, dc=dc: h.scalar_tensor_tensor(
                    out=dst_fn(dc, tsl), in0=hv(t, dc, tsl), scalar=cvec[:, gcol + dc:gcol + dc + 1],
                    in1=rs_c[tb2], op0=ALU.mult, op1=ALU.mult),
                    reads=[hk(t, dc, tb2), ("S5", "rsc", tb2), ("C", "cvec")],
                    writes=[dst_keys_fn(dc, tb2)])
                yield

        def e_half_gen(t, tb2, pre_bank=None):
            yield from rms_stats_gen(t, tb2, 32, lambda dc, tsl, t=t: hv(t, dc, tsl),
                                     lambda dc, tb2, t=t: hk(t, dc, tb2), pre_bank=pre_bank)
            t0 = t * 1024 + tb2 * 512
            for j in range(4):
                if t == 0:
                    src = hbuf[:, 4 * j:4 * j + 4, tb2 * 512:(tb2 + 1) * 512]
                else:
                    src = (h1a if j < 2 else h1b)[:, 4 * (j % 2):4 * (j % 2) + 4, tb2 * 512:(tb2 + 1) * 512]
                dma("sp", yT_v[:, 4 * j:4 * j + 4, t0:t0 + 512], src, f"d_out{j}",
                    reads=[hk(t, dc, tb2) for dc in range(4 * j, 4 * j + 4)])
                yield

        def e_half(t, tb2, pre_bank=None):
            for _ in e_half_gen(t, tb2, pre_bank=pre_bank):
                pass

        sq_ctr = [0]
        e0_gen = [None]
        r0_barrier_done = [False]

        def e0_steps(n):
            g = e0_gen[0]
            if g is None:
                return
            for _ in range(n):
                if next(g, "done") == "done":
                    e0_gen[0] = None
                    return

        xr_done = set()

        def issue_xr(t, db):
            if (t, db) in xr_done:
                return
            xr_done.add((t, db))
            k = db % 2
            dma("sp", xr[k], xT_v[:, db, t * 1024:(t + 1) * 1024], f"d_xr{k}", writes=[("S5a", "xr", k)])

        for t in range(2):
            mreg = "R1" if t == 0 else "R2"
            t0 = t * 1024
            if t == 0:
                tr.barrier("S5a")
            if t == 1:
                tr.barrier("R1")
                tr.barrier("R3")
            sbank = [next_bank(), next_bank()]
            reserved.update(sbank)
            pend_stats = []
            pend_hn = []

            def emit_stat(db, tb2, k):
                bb = sbank[tb2]
                tr.emit("pe", lambda h, k=k, bb=bb, db=db: h.matmul(psum[bb][:, :], ones, sqc[k], start=(db == 0), stop=(db == 15)),
                        reads=[("S5", "sqc", k), ("C", "ones")], writes=[("PS", bb)])

            def emit_hn(db, tb2):
                if t == 1 and not r0_barrier_done[0]:
                    e0_steps(10 ** 6)
                    tr.barrier("R0")
                    r0_barrier_done[0] = True
                tsl = slice(tb2 * 512, (tb2 + 1) * 512)
                tr.emit("act", lambda h, db=db, tsl=tsl, t=t: h.activation(
                    out=hnv(t, db, tsl), in_=hv(t, db, tsl), func=AF.Copy, scale=cvec[:, 16 + db:17 + db]),
                    reads=[hk(t, db, tb2), ("C", "cvec")], writes=[hnk(t, db, tb2)])

            for db in range(16):
                s = take_weight()
                k = db % 2
                issue_xr(t, db)
                for tb2 in range(2):
                    b = next_bank()
                    tsl = slice(tb2 * 512, (tb2 + 1) * 512)
                    mm_group(b, [(psum[b][:, :], wslot[s][:, ec, :], mix[t][:, ec, tsl], ec == 0, ec == 15)
                                 for ec in range(16)],
                             reads=[("R4", "w", s)] + [(mreg, "mix", ec, tb2) for ec in range(16)])
                    if pend_stats:
                        emit_stat(*pend_stats.pop(0))
                    if t == 1:
                        e0_steps(5)
                    tr.emit("dve", lambda h, b=b, db=db, tsl=tsl, k=k, t=t: h.tensor_tensor(
                        out=hv(t, db, tsl), in0=psum[b][:, :], in1=xr[k][:, tsl], op=ALU.add),
                        reads=[("PS", b), ("S5a", "xr", k)], writes=[hk(t, db, tb2)])
                    ks = sq_ctr[0] % 2
                    sq_ctr[0] += 1
                    tr.emit("act", lambda h, db=db, tsl=tsl, ks=ks, t=t: h.activation(
                        out=sqc[ks], in_=hv(t, db, tsl), func=AF.Square),
                        reads=[hk(t, db, tb2)], writes=[("S5", "sqc", ks)])
                    pend_stats.append((db, tb2, ks))
                    if t == 0:
                        emit_hn(db, tb2)
                    else:
                        pend_hn.append((db, tb2))
                if t == 1 and db >= 9:
                    for _ in range(5):
                        if pend_hn:
                            emit_hn(*pend_hn.pop(0))
            while pend_stats:
                emit_stat(*pend_stats.pop(0))
            if t == 1:
                e0_steps(10 ** 6)
            while pend_hn:
                emit_hn(*pend_hn.pop(0))
            for tb2 in range(2):
                tr.emit("dve", lambda h, tb2=tb2, sb=sbank[tb2]: h.tensor_scalar(
                    out=rs_c[tb2], in0=psum[sb][:, :], scalar1=EPS, scalar2=None, op0=ALU.add),
                    reads=[("PS", sbank[tb2])], writes=[("S5", "rsc", tb2)])
                tr.emit("dve", lambda h, tb2=tb2: h.reciprocal(out=rs_c[tb2], in_=rs_c[tb2]),
                        reads=[("S5", "rsc", tb2)], writes=[("S5", "rsc", tb2)])
            reserved.difference_update(sbank)
            if t == 0:
                dump("hC", hbuf)
            if t == 0:
                tr.barrier("R1")
            tr.barrier("S5a")
            it = 0
            for q in range(4):
                for fc in range(16):
                    s = take_weight()
                    for tb2 in range(2):
                        b = next_bank()
                        k = it % 2
                        it += 1
                        tsl = slice(tb2 * 512, (tb2 + 1) * 512)
                        mm_group(b, [(psum[b][:, :], wslot[s][:, dc, :], hnv(t, dc, tsl), dc == 0, dc == 15)
                                     for dc in range(16)],
                                 reads=[("R4", "w", s)] + [hnk(t, dc, tb2) for dc in range(16)])
                        tr.emit("act", lambda h, b=b, k=k: h.activation(out=rl[k], in_=psum[b][:, :], func=AF.Relu),
                                reads=[("PS", b)], writes=[("S5a", "rl", k)])
                        tr.emit("act", lambda h, k=k: h.activation(out=rl[k], in_=rl[k], func=AF.Square),
                                reads=[("S5a", "rl", k)], writes=[("S5a", "rl", k)])
                        tr.emit("dve", lambda h, k=k, fc=fc, tsl=tsl, t=t, tb2=tb2: h.tensor_tensor(
                            out=rv(t, fc, tsl), in0=rl[k], in1=rs_c[tb2], op=ALU.mult),
                            reads=[("S5a", "rl", k), ("S5", "rsc", tb2)], writes=[rk(t, fc, tb2)])
                def down_group(s, db, tb2):
                    b = next_bank()
                    tsl = slice(tb2 * 512, (tb2 + 1) * 512)
                    mm_group(b, [(psum[b][:, :], wslot[s][:, fc, :], rv(t, fc, tsl), fc == 0, fc == 15)
                                 for fc in range(16)],
                             reads=[("R4", "w", s)] + [rk(t, fc, tb2) for fc in range(16)])
                    tr.emit("dve", lambda h, b=b, db=db, tsl=tsl, t=t: h.tensor_tensor(
                        out=hv(t, db, tsl), in0=psum[b][:, :], in1=hv(t, db, tsl), op=ALU.add),
                        reads=[("PS", b), hk(t, db, tb2)], writes=[hk(t, db, tb2)])

                if t == 1 and q == 3:
                    def last_pass(tb2, hook=None):
                        fbk = next_bank()
                        reserved.add(fbk)
                        pend = None
                        tslf = slice(tb2 * 512, (tb2 + 1) * 512)

                        def f_stat(db, ks):
                            tr.emit("pe", lambda h, ks=ks, db=db, fbk=fbk: h.matmul(
                                psum[fbk][:, :], ones, sqc[ks], start=(db == 0), stop=(db == 15)),
                                reads=[("S5", "sqc", ks), ("C", "ones")], writes=[("PS", fbk)])

                        for db in range(16):
                            s = take_weight()
                            down_group(s, db, tb2)
                            if pend is not None:
                                f_stat(*pend)
                                pend = None
                            if hook is not None and db >= 3:
                                hook(2 if db < 15 else 10 ** 6)
                            ks = sq_ctr[0] % 2
                            sq_ctr[0] += 1
                            tr.emit("act", lambda h, db=db, ks=ks, tslf=tslf: h.activation(
                                out=sqc[ks], in_=hv(1, db, tslf), func=AF.Square),
                                reads=[hk(1, db, tb2)], writes=[("S5", "sqc", ks)])
                            pend = (db, ks)
                        f_stat(*pend)
                        return fbk

                    fb0 = last_pass(0)
                    g10 = [None]

                    def hook10(n):
                        if g10[0] is None:
                            g10[0] = e_half_gen(1, 0, pre_bank=fb0)
                        for _ in range(n):
                            if next(g10[0], "done") == "done":
                                return

                    fb1 = last_pass(1, hook=hook10)
                    e_half(1, 1, pre_bank=fb1)
                else:
                    for db in range(16):
                        s = take_weight()
                        for tb2 in range(2):
                            down_group(s, db, tb2)
            if t == 0:
                dump("hD", hbuf)
            if t == 0:
                tr.barrier("S5a")
                issue_xr(1, 0)
                issue_xr(1, 1)
                import itertools
                e0_gen[0] = itertools.chain(e_half_gen(0, 0), e_half_gen(0, 1))

        for j in range(4):
            tr.prog["sp"].append(("wait", f"d_out{j}", tr.count[f"d_out{j}"]))

        _check_no_deadlock(tr)
        semkeys = sorted(tr.count.keys())
        sems = {k: es.enter_context(nc.semaphore(k)) for k in semkeys}

        def replay(eng, h):
            for item in tr.prog[eng]:
                if item[0] == "wait":
                    h.wait_ge(sems[item[1]], item[2])
                else:
                    _, fn, sk, inc = item
                    ins = fn(h)
                    if sk is not None:
                        ins.then_inc(sems[sk], inc)

        with nc.Block() as block:
            @block.sync
            def _(h):
                replay("sp", h)

            @block.gpsimd
            def _(h):
                replay("pool", h)

            @block.tensor
            def _(h):
                replay("pe", h)

            @block.scalar
            def _(h):
                replay("act", h)

            @block.vector
            def _(h):
                replay("dve", h)
    nc._dbg_outs = dbg_outs
    return nc


def _tile_w(w, nblk):
    K, N = w.shape
    j = N // nblk
    t = w.reshape(K // 128, 128, nblk, j).transpose(2, 1, 0, 3)
    return np.ascontiguousarray(t).reshape(nblk, 128, (K // 128) * j)


def _const_tables():
    bf = ml_dtypes.bfloat16
    m = np.arange(128)
    ang = 2.0 * np.pi * ((m[:, None] * m[None, :]) % 128).astype(np.float64) / 128.0
    ccsc = np.zeros((128, 272), dtype=np.float64)
    ccsc[:, 0:128] = np.cos(ang) / 512.0
    ccsc[:, 128:256] = np.sin(ang) / 512.0
    ccsc[:, 256] = 1.0 - 2.0 * (m % 2)
    s = np.arange(S)
    k = np.arange(S // 2)
    angs = 2.0 * np.pi * ((s[:, None] * k[None, :]) % S).astype(np.float64) / S
    tabs = np.stack([np.cos(angs), np.sin(angs)], axis=0)
    tabs = tabs.reshape(2, 16, 128, 2, 512).transpose(3, 2, 0, 1, 4)
    tabs = np.ascontiguousarray(tabs).reshape(2, 128, 2 * 16 * 512)
    return ccsc.astype(np.float32).astype(bf), tabs.astype(np.float32).astype(bf)


_CACHE = {}


def _prep(x, norm_mix_g, w_in, fourier_w, gmlp_v_g, gmlp_ws, gmlp_b, w_out,
          norm_mlp_g, w_up, w_down, norm_final_g):
    x = np.asarray(x, dtype=np.float32)
    f = lambda a: np.asarray(a, dtype=np.float32)
    w_in, w_out, w_up, w_down = f(w_in), f(w_out), f(w_up), f(w_down)
    if "tables" not in _CACHE:
        _CACHE["tables"] = _const_tables()
    ccsc, tabs = _CACHE["tables"]
    wfm = _tile_w(w_in[:, :2048], 16)
    wv = _tile_w(w_in[:, 2048:], 2)
    wout = _tile_w(w_out, 16)
    wup = _tile_w(w_up, 64)
    wdn = np.ascontiguousarray(
        w_down.reshape(4, 16, 128, 16, 128).transpose(0, 3, 2, 1, 4)).reshape(64, 128, 2048)
    cvec = np.concatenate([f(norm_mix_g).reshape(16, 128).T, f(norm_mlp_g).reshape(16, 128).T,
                           f(norm_final_g).reshape(16, 128).T, f(gmlp_v_g).T], axis=1)
    cvec = np.ascontiguousarray(cvec, dtype=np.float32)
    gb = np.ascontiguousarray(f(gmlp_b))
    wsT = np.ascontiguousarray(f(gmlp_ws).transpose(2, 0, 1)).reshape(128, 1024)
    wf = np.ascontiguousarray(f(fourier_w).transpose(1, 0, 2)).reshape(128, 1024)
    shared = {"wfm": wfm, "wv": wv, "wout": wout, "wup": wup, "wdn": wdn, "cvec": cvec, "gb": gb,
              "wsT": wsT, "wf": wf, "ccsc": ccsc, "tabs": tabs}
    in_maps = []
    for b in range(NCORES):
        m = dict(shared)
        m["xT"] = np.ascontiguousarray(x[b].T)
        in_maps.append(m)
    return in_maps


def kernel(x, norm_mix_g, w_in, fourier_w, gmlp_v_g, gmlp_ws, gmlp_b, w_out,
           norm_mlp_g, w_up, w_down, norm_final_g):
    if "nc" not in _CACHE:
        _CACHE["nc"] = build_nc()
    nc = _CACHE["nc"]
    in_maps = _prep(x, norm_mix_g, w_in, fourier_w, gmlp_v_g, gmlp_ws, gmlp_b, w_out,
                    norm_mlp_g, w_up, w_down, norm_final_g)
    res = run_bass_kernel_spmd(nc, in_maps, core_ids=list(range(NCORES)))
    out = np.empty((NCORES, S, D), dtype=np.float32)
    for b in range(NCORES):
        out[b] = np.asarray(res.results[b]["yT"]).T
    return out
```

```python
import math
from contextlib import ExitStack

import ml_dtypes
import numpy as np

import concourse.bass as bass
import concourse.mybir as mybir
from concourse.bass_utils import run_bass_kernel_spmd

F32 = mybir.dt.float32
BF16 = mybir.dt.bfloat16
U8 = mybir.dt.uint8
AF = mybir.ActivationFunctionType
ALU = mybir.AluOpType
AX = mybir.AxisListType

D = 2048
S = 2048
NCH = 16
DFF = 8192
EPS = 1e-6
NCORES = 8
NSLOT = 8

R0 = 0
R1 = 65536
R2 = 98304
R3 = 131072
R4 = 163840
R5 = 196608
CVEC_OFF = R5
ONES_OFF = R5 + 224
ONEV_OFF = R5 + 480
RSTD_OFF = R5 + 512
S5 = R5 + 1024
ARENA = 212736
S5_SIZE = ARENA - S5


class Tracker:
    def __init__(self):
        self.prog = {}
        self.count = {}
        self.seen = {}
        self.state = {}
        self.region_prev = {}
        self.engsem = {}

    def add_engine(self, eng, semkey):
        self.prog[eng] = []
        self.seen[eng] = {}
        self.engsem[eng] = semkey
        self.count.setdefault(semkey, 0)

    def _st(self, key):
        if key not in self.state:
            self.state[key] = {"w": {}, "r": {}}
        return self.state[key]

    def barrier(self, region):
        prev = self.region_prev.setdefault(region, {})
        for key, st in self.state.items():
            if key[0] == region:
                for d in (st["w"], st["r"]):
                    for s, v in d.items():
                        if prev.get(s, 0) < v:
                            prev[s] = v

    def emit(self, eng, fn, reads=(), writes=(), dma_sem=None, inc=True):
        deps = {}

        def add(d):
            for s, v in d.items():
                if deps.get(s, 0) < v:
                    deps[s] = v

        own = self.engsem[eng]
        for k in reads:
            add(self._st(k)["w"])
        for k in writes:
            st = self._st(k)
            skip = own if eng == "pe" else None
            add({s: v for s, v in st["w"].items() if s != skip})
            add({s: v for s, v in st["r"].items() if s != skip})
            add({s: v for s, v in self.region_prev.get(k[0], {}).items() if s != skip})
        seen = self.seen[eng]
        for s, v in deps.items():
            if seen.get(s, 0) < v:
                self.prog[eng].append(("wait", s, v))
                seen[s] = v
        if dma_sem is not None:
            self.count.setdefault(dma_sem, 0)
            self.count[dma_sem] += 16
            tok = (dma_sem, self.count[dma_sem])
            self.prog[eng].append(("op", fn, dma_sem, 16))
        elif inc:
            self.count[own] += 1
            tok = (own, self.count[own])
            self.prog[eng].append(("op", fn, own, 1))
        else:
            tok = None
            self.prog[eng].append(("op", fn, None, 0))
        if tok is not None:
            for k in reads:
                d = self._st(k)["r"]
                if d.get(tok[0], 0) < tok[1]:
                    d[tok[0]] = tok[1]
            for k in writes:
                d = self._st(k)["w"]
                if d.get(tok[0], 0) < tok[1]:
                    d[tok[0]] = tok[1]
        return tok


def _check_no_deadlock(tr):
    pc = {e: 0 for e in tr.prog}
    sem = {}
    progress = True
    while progress:
        progress = False
        for e, items in tr.prog.items():
            while pc[e] < len(items):
                it = items[pc[e]]
                if it[0] == "wait":
                    if sem.get(it[1], 0) < it[2]:
                        break
                else:
                    if it[2] is not None:
                        sem[it[2]] = sem.get(it[2], 0) + it[3]
                pc[e] += 1
                progress = True
    stuck = {e: (pc[e], len(items)) for e, items in tr.prog.items() if pc[e] < len(items)}
    assert not stuck, f"deadlock in recorded program: {stuck}"
    for k, v in tr.count.items():
        assert sem.get(k, 0) == v, (k, sem.get(k, 0), v)


def build_nc(debug=False):
    nc = bass.Bass("TRN2", target_bir_lowering=False)
    dbg_outs = []
    xT = nc.dram_tensor("xT", [D, S], F32, kind="ExternalInput").ap()
    wfm = nc.dram_tensor("wfm", [16, 128, 2048], F32, kind="ExternalInput").ap()
    wv = nc.dram_tensor("wv", [2, 128, 8192], F32, kind="ExternalInput").ap()
    wout = nc.dram_tensor("wout", [16, 128, 2048], F32, kind="ExternalInput").ap()
    wup = nc.dram_tensor("wup", [64, 128, 2048], F32, kind="ExternalInput").ap()
    wdn = nc.dram_tensor("wdn", [64, 128, 2048], F32, kind="ExternalInput").ap()
    cvec_d = nc.dram_tensor("cvec", [128, 56], F32, kind="ExternalInput").ap()
    gb_d = nc.dram_tensor("gb", [8, 128], F32, kind="ExternalInput")
    wsT_d = nc.dram_tensor("wsT", [128, 1024], F32, kind="ExternalInput").ap()
    wf_d = nc.dram_tensor("wf", [128, 1024], F32, kind="ExternalInput").ap()
    ccsc_d = nc.dram_tensor("ccsc", [128, 272], BF16, kind="ExternalInput").ap()
    tabs_d = nc.dram_tensor("tabs", [2, 128, 16384], BF16, kind="ExternalInput").ap()
    yT = nc.dram_tensor("yT", [D, S], F32, kind="ExternalOutput").ap()

    xT_v = xT.rearrange("(c p) t -> p c t", p=128)
    yT_v = yT.rearrange("(c p) t -> p c t", p=128)

    tr = Tracker()
    for eng in ("pe", "act", "dve", "pool", "sp"):
        tr.add_engine(eng, eng)

    with ExitStack() as es:
        arena = es.enter_context(nc.sbuf_tensor("arena", [128, ARENA], U8))
        psum = [es.enter_context(nc.psum_tensor(f"ps{i}", [128, 512], F32)) for i in range(8)]

        def view(off, nbytes, dt, pat=None, **kw):
            v = arena[:, off:off + nbytes].bitcast(dt)
            if pat is not None:
                v = v.rearrange(pat, **kw)
            return v

        cvec = view(CVEC_OFF, 224, F32)
        ones = view(ONES_OFF, 256, BF16)
        epsb = view(ONEV_OFF, 4, F32)
        rstd_all = view(RSTD_OFF, 512, F32)
        xn = view(R0, 65536, BF16, "p (c t) -> p c t", t=S)
        Abuf = view(R0, 65536, BF16, "p (t g m) -> p t g m", g=8, m=256)
        hbuf = view(R0, 65536, F32, "p (c t) -> p c t", t=1024)
        mix = [view(R1, 32768, BF16, "p (c t) -> p c t", t=1024),
               view(R2, 32768, BF16, "p (c t) -> p c t", t=1024)]
        rbuf = view(R1, 32768, BF16, "p (c t) -> p c t", t=1024)
        xs = [view(R3 + i * 16384, 16384, F32, "p (c t) -> p c t", t=256) for i in range(2)]
        vtok = view(R3, 32768, BF16, "p (t e) -> p t e", e=1024)
        hn = view(R3, 32768, BF16, "p (c t) -> p c t", t=1024)
        tab = [view(R3, 32768, BF16, "p (a s k) -> p a s k", a=2, k=512),
               view(R4, 32768, BF16, "p (a s k) -> p a s k", a=2, k=512)]
        wslot = [view(R4 + i * 4096, 4096, BF16, "p (c j) -> p c j", j=128) for i in range(NSLOT)]
        wvslot = [view(R4 + i * 16384, 16384, BF16, "p (c j) -> p c j", j=512) for i in range(2)]

        bank_ctr = [0]
        reserved = set()

        def next_bank():
            while True:
                b = bank_ctr[0] % 8
                bank_ctr[0] += 1
                if b not in reserved:
                    return b

        def dma(eng, out, in_, sem, reads=(), writes=()):
            tr.emit(eng, lambda h, o=out, i=in_: h.dma_start(out=o, in_=i),
                    reads=reads, writes=writes, dma_sem=sem)

        def dump(name, src):
            if not debug:
                return
            shp = list(src.shape)
            dt = nc.dram_tensor("dbg_" + name, shp, src.dtype, kind="ExternalOutput").ap()
            dbg_outs.append("dbg_" + name)
            sk = "d_dbg_" + name
            tr.count[sk] = 16
            for s_, v_ in tr.count.items():
                if s_ != sk and tr.seen["sp"].get(s_, 0) < v_:
                    tr.prog["sp"].append(("wait", s_, v_))
                    tr.seen["sp"][s_] = v_
            tr.prog["sp"].append(("op", lambda h, o=dt, i=src: h.dma_start(out=o, in_=i), sk, 16))
            for e in tr.prog:
                tr.prog[e].append(("wait", sk, 16))
                tr.seen[e][sk] = 16

        def mm_group(bank, mms, reads, extra_writes=()):
            n = len(mms)

            def fn(h, mms=mms):
                last = None
                for (o, l, r, st, sp) in mms:
                    last = h.matmul(o, l, r, start=st, stop=sp)
                return last
            tr.emit("pe", fn, reads=reads, writes=[("PS", bank)] + list(extra_writes))

        wq = []
        wq.append(("v", wv[0]))
        wq.append(("v", wv[1]))
        EB_ORDER = list(range(8, 16)) + list(range(8))
        for eb in EB_ORDER:
            wq.append(("std", wfm[eb]))
        for t in range(2):
            for db in range(16):
                wq.append(("std", wout[db]))
            for q in range(4):
                for fc in range(16):
                    wq.append(("std", wup[q * 16 + fc]))
                for rep in range(2 if (t == 1 and q == 3) else 1):
                    for db in range(16):
                        wq.append(("std", wdn[q * 16 + db]))
        w_issued = [0]
        w_limit = [1]
        w_slot_of = {}
        slot_ctr = [0]

        slot_tile = [-1] * NSLOT
        w_extra_reads = []

        xslot = [view(R1 + i * 4096, 4096, BF16, "p (c j) -> p c j", j=128) for i in range(4)]
        XT0 = 2

        def wtile(sid):
            return wslot[sid] if sid < 100 else xslot[sid - 100]

        def wkey(sid):
            return ("R4", "w", sid) if sid < 100 else ("R1", "wx", sid - 100)

        def issue_x_tiles():
            for j in range(4):
                i = XT0 + j
                assert w_issued[0] == i
                dma("pool", xslot[j].rearrange("p c j -> p (c j)"), wq[i][1], f"d_wx{j}",
                    writes=[("R1", "wx", j)])
                w_slot_of[i] = 100 + j
                w_issued[0] += 1

        def issue_weights(upto, cur):
            while w_issued[0] <= min(upto, len(wq) - 1, w_limit[0]):
                i = w_issued[0]
                kind, src = wq[i]
                if kind == "std":
                    s = slot_ctr[0] % NSLOT
                    need = [s]
                    adv = 1
                else:
                    pad = (-slot_ctr[0]) % 4
                    s = (slot_ctr[0] + pad) % NSLOT
                    need = [s + j for j in range(4)]
                    adv = pad + 4
                if any(slot_tile[q] >= cur for q in need):
                    break
                slot_ctr[0] += adv
                for q in need:
                    slot_tile[q] = i
                if kind == "std":
                    dst = wslot[s].rearrange("p c j -> p (c j)")
                else:
                    dst = wvslot[s // 4].rearrange("p c j -> p (c j)")
                keys = [("R4", "w", q) for q in need]
                w_slot_of[i] = s
                dma("pool", dst, src, f"d_w{s}", reads=list(w_extra_reads), writes=keys)
                w_issued[0] += 1

        LOOK = 6
        w_next = [0]

        def take_weight():
            i = w_next[0]
            w_next[0] += 1
            issue_weights(i + LOOK, i)
            assert i in w_slot_of, i
            return w_slot_of[i]

        dma("sp", cvec, cvec_d, "d_c0", writes=[("C", "cvec")])
        tr.emit("dve", lambda h: h.memset(ones, 1.0 / D), writes=[("C", "ones")])
        tr.emit("dve", lambda h: h.memset(epsb, EPS), writes=[("C", "eps")])
        w_limit[0] = 0
        sv0 = take_weight()
        wvkeys = [[("R4", "w", sv0 + j) for j in range(4)], None]

        xs = [view(R1 + i * 16384, 16384, F32, "p (c t) -> p c t", t=256) for i in range(2)]
        rs_a = [view(S5 + i * 1024, 1024, F32) for i in range(2)]
        sq16 = view(S5 + 2048, 8192, BF16, "p (c t) -> p c t", t=256)
        sqv = [view(S5 + 10240 + i * 2048, 2048, F32) for i in range(2)]
        ss_all = view(S5 + 14336, 512, F32)
        it = 0

        def a0_sq(tb):
            sl = tb % 2
            tsl = slice(tb * 256, (tb + 1) * 256)
            dma("sp", xs[sl], xT_v[:, :, tsl], f"d_xs{sl}", reads=(wvkeys[0] if tb == 1 else []),
                writes=[("R1", "xs", sl)])
            tr.emit("act", lambda h, sl=sl: h.activation(out=sq16, in_=xs[sl], func=AF.Square),
                    reads=[("R1", "xs", sl)], writes=[("S5", "sq16")])

        def a0_rest(tb):
            sl = tb % 2
            tsl = slice(tb * 256, (tb + 1) * 256)
            b = next_bank()
            mm_group(b, [(psum[b][:, 0:256], ones, sq16[:, c, :], c == 0, c == 15) for c in range(16)],
                     reads=[("S5", "sq16"), ("C", "ones")])
            tr.emit("act", lambda h, b=b, sl=sl: h.activation(
                out=rs_a[sl], in_=psum[b][:, 0:256], func=AF.Sqrt, bias=epsb[:, 0:1], scale=1.0),
                reads=[("PS", b), ("C", "eps")], writes=[("S5", "rs", sl)])
            tr.emit("dve", lambda h, sl=sl: h.reciprocal(out=rs_a[sl], in_=rs_a[sl]),
                    reads=[("S5", "rs", sl)], writes=[("S5", "rs", sl)])
            for c in range(16):
                tr.emit("dve", lambda h, c=c, sl=sl, tsl=tsl: h.scalar_tensor_tensor(
                    out=xn[:, c, tsl], in0=xs[sl][:, c, :], scalar=cvec[:, c:c + 1], in1=rs_a[sl],
                    op0=ALU.mult, op1=ALU.mult),
                    reads=[("R1", "xs", sl), ("S5", "rs", sl), ("C", "cvec")],
                    writes=[("R0", "xn", tb, c)])

        def a2_block(tb):
            nonlocal_it = a2_it
            for cb in range(2):
                for tt in (2 * tb, 2 * tb + 1):
                    b = next_bank()
                    k = nonlocal_it[0] % 2
                    nonlocal_it[0] += 1
                    mm_group(b, [(psum[b][:, :], xn[:, c, tt * 128:(tt + 1) * 128], wvv[cb][:, c, :], c == 0, c == 15)
                                 for c in range(16)],
                             reads=wvkeys[cb] + [("R0", "xn", tb, c) for c in range(16)])
                    vblk = vtok[:, tt, cb * 512:(cb + 1) * 512]
                    tr.emit("act", lambda h, b=b, vblk=vblk: h.activation(out=vblk, in_=psum[b][:, :], func=AF.Gelu),
                            reads=[("PS", b)], writes=[("R3", "v", tt, cb)])
                    tr.emit("dve", lambda h, k=k, vblk=vblk: h.tensor_tensor(out=sqv[k], in0=vblk, in1=vblk, op=ALU.mult),
                            reads=[("R3", "v", tt, cb)], writes=[("S5", "sqv", k)])
                    c0 = tt * 8 + cb * 4
                    tr.emit("dve", lambda h, k=k, c0=c0: h.tensor_reduce(
                        out=ss_all[:, c0:c0 + 4], in_=sqv[k].rearrange("p (a d) -> p a d", d=128),
                        axis=AX.X, op=ALU.add),
                        reads=[("S5", "sqv", k)], writes=[("S5", "ss_all", c0)])

        a2_it = [0]
        a0_sq(0)
        w_limit[0] = 1
        w_extra_reads.append(("R1", "xs", 0))
        sv1 = take_weight()
        w_extra_reads.clear()
        wvkeys[1] = [("R4", "w", sv1 + j) for j in range(4)]
        wvv = [wvslot[sv0 // 4], wvslot[sv1 // 4]]
        a0_rest(0)
        a0_sq(1)
        for tb in range(8):
            if tb + 1 < 8:
                a0_rest(tb + 1)
                if tb + 1 == 7:
                    tr.barrier("R1")
                    issue_x_tiles()
            if tb + 2 < 8:
                a0_sq(tb + 2)
            a2_block(tb)
        tr.emit("act", lambda h: h.activation(out=rstd_all, in_=ss_all, func=AF.Sqrt, bias=epsb[:, 0:1],
                                              scale=1.0 / 128.0),
                reads=[("S5", "ss_all", tt * 8 + cb * 4) for tt in range(16) for cb in range(2)] + [("C", "eps")],
                writes=[("C", "rstd_all")])
        tr.emit("dve", lambda h: h.reciprocal(out=rstd_all, in_=rstd_all),
                reads=[("C", "rstd_all")], writes=[("C", "rstd_all")])
        w_limit[0] = 17
        dump("xn", xn)
        dump("vtok", vtok)
        dump("rstd", rstd_all)

        tr.barrier("R1")
        tr.barrier("S5")
        bbc = view(S5, 4096, F32, "p (h q) -> p h q", q=128)
        wsT = view(S5 + 4096, 2048, BF16, "p (h q) -> p h q", q=128)
        tb3 = [view(S5 + 6144, 2048, F32)] * 2
        wfb = view(S5 + 8192, 2048, BF16, "p (g d) -> p g d", d=128)
        ccsc = view(S5 + 14336, 544, BF16)
        Mg = view(S5, 4096, BF16, "p (g m) -> p g m", m=256)
        dma("sp", ccsc, ccsc_d, "d_c3", writes=[("S5", "ccsc")])
        dma("pool", wfb.rearrange("p g d -> p (g d)"), wf_d, "d_c4", writes=[("S5", "wfb")])
        NWS = 16
        wsS = [view(S5 + 10240 + i * 256, 256, BF16) for i in range(NWS)]
        dma("sp", bbc, bass.AP(gb_d, 0, [[0, 128], [128, 8], [1, 128]]), "d_c1", writes=[("S5", "bbc")])
        dma("pool", wsT.rearrange("p h q -> p (h q)"), wsT_d, "d_c2", writes=[("S5", "wsT")])
        b3_it = [0]
        b3_iw = [0]

        def gmlp_scale(hd):
            for tt in range(16):
                col = tt * 8 + hd
                tr.emit("act", lambda h, tt=tt, hd=hd, col=col: h.activation(
                    out=wsS[tt], in_=wsT[:, hd, :], func=AF.Copy, scale=rstd_all[:, col:col + 1]),
                    reads=[("S5", "wsT"), ("C", "rstd_all")], writes=[("S5", "wsS", tt)])

        def gmlp_head(hd):
            for tb in range(4):
                b = next_bank()
                k = 0
                mms = []
                rd = []
                for j in range(4):
                    tt = tb * 4 + j
                    kw = tt
                    mms.append((psum[b][:, j * 128:(j + 1) * 128], vtok[:, tt, hd * 128:(hd + 1) * 128],
                                wsS[kw], True, True))
                    rd += [("S5", "wsS", kw), ("R3", "v", tt, hd // 4)]
                mm_group(b, mms, reads=rd)
                th, off = tb // 2, (tb % 2) * 512
                ublk = mix[th][:, 8 + hd, off:off + 512]
                key = ("R1" if th == 0 else "R2", "mix", 8 + hd, tb % 2)
                tr.emit("dve", lambda h, b=b, k=k, hd=hd: h.scalar_tensor_tensor(
                    out=tb3[k].rearrange("p (a q) -> p a q", q=128),
                    in0=psum[b][:, :].rearrange("p (a q) -> p a q", q=128),
                    scalar=cvec[:, 48 + hd:49 + hd],
                    in1=bbc[:, hd, :].unsqueeze(1).to_broadcast([128, 4, 128]),
                    op0=ALU.mult, op1=ALU.add),
                    reads=[("PS", b), ("S5", "bbc"), ("C", "cvec")], writes=[("S5", "t3", k)])
                tr.emit("dve", lambda h, k=k, ublk=ublk: h.tensor_tensor(
                    out=ublk, in0=tb3[k], in1=ublk, op=ALU.mult),
                    reads=[("S5", "t3", k), key], writes=[key])

        for i, eb in enumerate(EB_ORDER):
            s = take_weight()
            if i == 8:
                tr.barrier("R1")
            if 1 <= i <= 8:
                gmlp_scale(i - 1)
            for tb in range(4):
                b = next_bank()
                tsl = slice(tb * 512, (tb + 1) * 512)
                mm_group(b, [(psum[b][:, :], wtile(s)[:, c, :], xn[:, c, tsl], c == 0, c == 15)
                             for c in range(16)],
                         reads=[wkey(s)] + [("R0", "xn", 2 * tb + u, c) for u in range(2) for c in range(16)])
                th, off = tb // 2, (tb % 2) * 512
                dst = mix[th][:, eb, off:off + 512]
                key = ("R1" if th == 0 else "R2", "mix", eb, tb % 2)
                if eb < 8:
                    tr.emit("dve", lambda h, b=b, dst=dst: h.tensor_copy(out=dst, in_=psum[b][:, :]),
                            reads=[("PS", b)], writes=[key])
                else:
                    tr.emit("act", lambda h, b=b, dst=dst: h.activation(out=dst, in_=psum[b][:, :], func=AF.Gelu),
                            reads=[("PS", b)], writes=[key])
            if 1 <= i <= 8:
                gmlp_head(i - 1)
            if i == 12:
                tr.barrier("S5")
                for g in range(8):
                    b = next_bank()
                    mm_group(b, [(psum[b][:, 0:128], ccsc[:, 0:128], wfb[:, g, :], True, True),
                                 (psum[b][:, 128:256], ccsc[:, 128:256], wfb[:, g, :], True, True)],
                             reads=[("S5", "ccsc"), ("S5", "wfb")])
                    tr.emit("dve", lambda h, b=b, g=g: h.tensor_copy(out=Mg[:, g, :], in_=psum[b][:, 0:256]),
                            reads=[("PS", b)], writes=[("S5", "Mg", g)])
            if i == 9:
                tr.barrier("R3")
                dma("sp", tab[0].rearrange("p a s k -> p (a s k)"), tabs_d[0], "d_tab0", writes=[("R3", "tab", 0)])
        dump("mixA0", mix[0])
        dump("mixA1", mix[1])

        tr.barrier("R0")
        tr.barrier("R4")
        dma("sp", tab[1].rearrange("p a s k -> p (a s k)"), tabs_d[1], "d_tab1", writes=[("R4", "tab", 1)])
        it = 0
        for tt in range(16):
            th, toff = tt // 8, (tt % 8) * 128
            for gp in range(4):
                b = next_bank()
                mms = []
                rd = []
                for gi in range(2):
                    g = 2 * gp + gi
                    rd.append(("S5", "Mg", g))
                    mms.append((psum[b][:, gi * 256:(gi + 1) * 256], mix[th][:, g, toff:toff + 128],
                                Mg[:, g, :], True, True))
                    rd.append(("R1" if th == 0 else "R2", "mix", g, (tt % 8) // 4))
                mm_group(b, mms, reads=rd)
                dst = Abuf[:, tt, 2 * gp:2 * gp + 2, :].rearrange("p g m -> p (g m)")
                if it % 2 == 0:
                    tr.emit("dve", lambda h, b=b, dst=dst: h.tensor_copy(out=dst, in_=psum[b][:, :]),
                            reads=[("PS", b)], writes=[("R0", "A", tt, gp)])
                else:
                    tr.emit("act", lambda h, b=b, dst=dst: h.activation(out=dst, in_=psum[b][:, :], func=AF.Copy),
                            reads=[("PS", b)], writes=[("R0", "A", tt, gp)])
                it += 1

        dump("A", Abuf)
        tr.barrier("S5")
        qs = [view(S5 + 8192 + i * 2048, 2048, F32) for i in range(2)]
        it = 0
        for kb in range(2):
            sl = kb
            reg = "R3" if sl == 0 else "R4"
            for g in range(8):
                bP = next_bank()
                bQ = next_bank()
                k = it % 2
                it += 1
                rdA = [(reg, "tab", sl)] + [("R0", "A", st, g // 2) for st in range(16)]
                mm_group(bP, [(psum[bP][:, :], Abuf[:, st, g, 0:128], tab[sl][:, 0, st, :], st == 0, st == 15)
                              for st in range(16)], reads=rdA)
                mm_group(bQ, [(psum[bQ][:, :], Abuf[:, st, g, 128:256], tab[sl][:, 1, st, :], st == 0, st == 15)
                              for st in range(16)], reads=rdA)
                tr.emit("act", lambda h, bQ=bQ, k=k: h.activation(out=qs[k], in_=psum[bQ][:, :], func=AF.Copy),
                        reads=[("PS", bQ)], writes=[("S5", "qs", k)])
                dst = mix[0][:, g, kb * 512:(kb + 1) * 512]
                tr.emit("dve", lambda h, bP=bP, k=k, dst=dst: h.tensor_tensor(
                    out=dst, in0=psum[bP][:, :], in1=qs[k], op=ALU.subtract),
                    reads=[("PS", bP), ("S5", "qs", k)], writes=[("R1", "mix", g, kb)])
                c0 = 1 if kb == 0 else 0
                n = 512 - c0
                first = 1024 - (kb * 512 + c0)
                fwd = mix[1][:, g, first:first + 1]
                rev = bass.AP(fwd.tensor, fwd.offset, [list(fwd.ap[0]), [-1, n]])
                wk = [("R2", "mix", g, 1)] if kb == 0 else [("R2", "mix", g, 0), ("R2", "mix", g, 1)]
                tr.emit("dve", lambda h, bP=bP, k=k, rev=rev, c0=c0: h.tensor_tensor(
                    out=rev, in0=psum[bP][:, c0:512], in1=qs[k][:, c0:512], op=ALU.add),
                    reads=[("PS", bP), ("S5", "qs", k)], writes=wk)
        bN = next_bank()
        mmsN = []
        for g in range(8):
            for st in range(16):
                mmsN.append((psum[bN][:, g:g + 1], Abuf[:, st, g, 0:128], ccsc[:, 256:257], st == 0, st == 15))
        mm_group(bN, mmsN, reads=[("S5", "ccsc")] + [("R0", "A", st, gp) for st in range(16) for gp in range(4)])
        tr.emit("dve", lambda h, bN=bN: h.tensor_copy(out=mix[1][:, 0:8, 0], in_=psum[bN][:, 0:8]),
                reads=[("PS", bN)], writes=[("R2", "mix", g, 0) for g in range(8)])

        dump("mixB2_0", mix[0])
        dump("mixB2_1", mix[1])
        tr.barrier("R4")
        w_limit[0] = len(wq)
        tr.barrier("R0")
        tr.barrier("S5")
        tr.region_prev["S5a"] = dict(tr.region_prev["S5"])
        xr = [view(S5 + i * 4096, 4096, F32) for i in range(2)]
        rl = [view(S5 + i * 4096, 2048, F32) for i in range(2)]
        rs_c = [view(S5 + 8192 + i * 2048, 2048, F32) for i in range(2)]
        sqc = [view(S5 + 12288 + i * 1024, 1024, BF16) for i in range(2)]
        h1a = view(R1, 32768, F32, "p (c t) -> p c t", t=1024)
        h1b = view(R3, 32768, F32, "p (c t) -> p c t", t=1024)
        r1 = view(R0, 32768, BF16, "p (c t) -> p c t", t=1024)
        hn1 = view(R0 + 32768, 32768, BF16, "p (c t) -> p c t", t=1024)

        def hv(t, dc, tsl):
            if t == 0:
                return hbuf[:, dc, tsl]
            return h1a[:, dc, tsl] if dc < 8 else h1b[:, dc - 8, tsl]

        def hk(t, dc, tb2):
            if t == 0:
                return ("R0", "h", dc, tb2)
            return ("R1" if dc < 8 else "R3", "h1", dc, tb2)

        def rv(t, fc, tsl):
            return (rbuf if t == 0 else r1)[:, fc, tsl]

        def rk(t, fc, tb2):
            return ("R1", "r", fc, tb2) if t == 0 else ("R0", "r1", fc, tb2)

        def hnv(t, dc, tsl):
            return (hn if t == 0 else hn1)[:, dc, tsl]

        def hnk(t, dc, tb2):
            return ("R3", "hn", dc, tb2) if t == 0 else ("R0", "hn1", dc, tb2)

        def rms_stats_gen(t, tb2, gcol, dst_fn, dst_keys_fn, pre_bank=None):
            tsl = slice(tb2 * 512, (tb2 + 1) * 512)
            if pre_bank is None:
                b = next_bank()
                reserved.add(b)
            else:
                b = pre_bank
            for dc in (range(16) if pre_bank is None else ()):
                k = sq_ctr[0] % 2
                sq_ctr[0] += 1
                tr.emit("act", lambda h, dc=dc, k=k: h.activation(out=sqc[k], in_=hv(t, dc, tsl), func=AF.Square),
                        reads=[hk(t, dc, tb2)], writes=[("S5", "sqc", k)])
                tr.emit("pe", lambda h, dc=dc, k=k, b=b: h.matmul(psum[b][:, :], ones, sqc[k], start=(dc == 0), stop=(dc == 15)),
                        reads=[("S5", "sqc", k), ("C", "ones")], writes=[("PS", b)])
                yield
            tr.emit("act", lambda h, b=b: h.activation(
                out=rs_c[tb2], in_=psum[b][:, :], func=AF.Sqrt, bias=epsb[:, 0:1], scale=1.0),
                reads=[("PS", b), ("C", "eps")], writes=[("S5", "rsc", tb2)])
            tr.emit("dve", lambda h: h.reciprocal(out=rs_c[tb2], in_=rs_c[tb2]),
                    reads=[("S5", "rsc", tb2)], writes=[("S5", "rsc", tb2)])
            reserved.discard(b)
            yield
            for dc in range(16):
                tr.emit("dve", lambda h, dc=dc: h.scalar_tensor_tensor(
                    out=dst_fn(dc, tsl), in0=hv(t, dc, tsl), scalar=cvec[:, gcol + dc:gcol + dc + 1],
                    in1=rs_c[tb2], op0=ALU.mult, op1=ALU.mult),
                    reads=[hk(t, dc, tb2), ("S5", "rsc", tb2), ("C", "cvec")],
                    writes=[dst_keys_fn(dc, tb2)])
                yield

        def e_half_gen(t, tb2, pre_bank=None):
            yield from rms_stats_gen(t, tb2, 32, lambda dc, tsl, t=t: hv(t, dc, tsl),
                                     lambda dc, tb2, t=t: hk(t, dc, tb2), pre_bank=pre_bank)
            t0 = t * 1024 + tb2 * 512
            for j in range(4):
                if t == 0:
                    src = hbuf[:, 4 * j:4 * j + 4, tb2 * 512:(tb2 + 1) * 512]
                else:
                    src = (h1a if j < 2 else h1b)[:, 4 * (j % 2):4 * (j % 2) + 4, tb2 * 512:(tb2 + 1) * 512]
                dma("sp", yT_v[:, 4 * j:4 * j + 4, t0:t0 + 512], src, f"d_out{j}",
                    reads=[hk(t, dc, tb2) for dc in range(4 * j, 4 * j + 4)])
                yield

        def e_half(t, tb2, pre_bank=None):
            for _ in e_half_gen(t, tb2, pre_bank=pre_bank):
                pass

        sq_ctr = [0]
        e0_banks = [None, None]
        e0_gen = [None]
        r0_barrier_done = [False]

        def e0_steps(n):
            g = e0_gen[0]
            if g is None:
                return
            for _ in range(n):
                if next(g, "done") == "done":
                    e0_gen[0] = None
                    return

        xr_done = set()

        def issue_xr(t, db):
            if (t, db) in xr_done:
                return
            xr_done.add((t, db))
            k = db % 2
            dma("sp", xr[k], xT_v[:, db, t * 1024:(t + 1) * 1024], f"d_xr{k}", writes=[("S5a", "xr", k)])

        for t in range(2):
            mreg = "R1" if t == 0 else "R2"
            t0 = t * 1024
            if t == 0:
                tr.barrier("S5a")
            if t == 1:
                tr.barrier("R1")
                tr.barrier("R3")
            sbank = [next_bank(), next_bank()]
            reserved.update(sbank)
            pend_stats = []
            pend_hn = []

            def emit_stat(db, tb2, k):
                bb = sbank[tb2]
                tr.emit("pe", lambda h, k=k, bb=bb, db=db: h.matmul(psum[bb][:, :], ones, sqc[k], start=(db == 0), stop=(db == 15)),
                        reads=[("S5", "sqc", k), ("C", "ones")], writes=[("PS", bb)])

            def emit_hn(db, tb2):
                if t == 1 and not r0_barrier_done[0]:
                    e0_steps(10 ** 6)
                    tr.barrier("R0")
                    r0_barrier_done[0] = True
                tsl = slice(tb2 * 512, (tb2 + 1) * 512)
                tr.emit("act", lambda h, db=db, tsl=tsl, t=t: h.activation(
                    out=hnv(t, db, tsl), in_=hv(t, db, tsl), func=AF.Copy, scale=cvec[:, 16 + db:17 + db]),
                    reads=[hk(t, db, tb2), ("C", "cvec")], writes=[hnk(t, db, tb2)])

            for db in range(16):
                s = take_weight()
                k = db % 2
                issue_xr(t, db)
                for tb2 in range(2):
                    b = next_bank()
                    tsl = slice(tb2 * 512, (tb2 + 1) * 512)
                    mm_group(b, [(psum[b][:, :], wslot[s][:, ec, :], mix[t][:, ec, tsl], ec == 0, ec == 15)
                                 for ec in range(16)],
                             reads=[("R4", "w", s)] + [(mreg, "mix", ec, tb2) for ec in range(16)])
                    if pend_stats:
                        emit_stat(*pend_stats.pop(0))
                    if t == 1:
                        e0_steps(5)
                    tr.emit("dve", lambda h, b=b, db=db, tsl=tsl, k=k, t=t: h.tensor_tensor(
                        out=hv(t, db, tsl), in0=psum[b][:, :], in1=xr[k][:, tsl], op=ALU.add),
                        reads=[("PS", b), ("S5a", "xr", k)], writes=[hk(t, db, tb2)])
                    ks = sq_ctr[0] % 2
                    sq_ctr[0] += 1
                    tr.emit("act", lambda h, db=db, tsl=tsl, ks=ks, t=t: h.activation(
                        out=sqc[ks], in_=hv(t, db, tsl), func=AF.Square),
                        reads=[hk(t, db, tb2)], writes=[("S5", "sqc", ks)])
                    pend_stats.append((db, tb2, ks))
                    if t == 0:
                        emit_hn(db, tb2)
                    else:
                        pend_hn.append((db, tb2))
                if t == 1 and db >= 9:
                    for _ in range(5):
                        if pend_hn:
                            emit_hn(*pend_hn.pop(0))
            while pend_stats:
                emit_stat(*pend_stats.pop(0))
            if t == 1:
                e0_steps(10 ** 6)
            while pend_hn:
                emit_hn(*pend_hn.pop(0))
            for tb2 in range(2):
                tr.emit("dve", lambda h, tb2=tb2, sb=sbank[tb2]: h.tensor_scalar(
                    out=rs_c[tb2], in0=psum[sb][:, :], scalar1=EPS, scalar2=None, op0=ALU.add),
                    reads=[("PS", sbank[tb2])], writes=[("S5", "rsc", tb2)])
                tr.emit("dve", lambda h, tb2=tb2: h.reciprocal(out=rs_c[tb2], in_=rs_c[tb2]),
                        reads=[("S5", "rsc", tb2)], writes=[("S5", "rsc", tb2)])
            reserved.difference_update(sbank)
            if t == 0:
                dump("hC", hbuf)
            if t == 0:
                tr.barrier("R1")
            tr.barrier("S5a")
            it = 0
            for q in range(4):
                for fc in range(16):
                    s = take_weight()
                    for tb2 in range(2):
                        b = next_bank()
                        k = it % 2
                        it += 1
                        tsl = slice(tb2 * 512, (tb2 + 1) * 512)
                        mm_group(b, [(psum[b][:, :], wslot[s][:, dc, :], hnv(t, dc, tsl), dc == 0, dc == 15)
                                     for dc in range(16)],
                                 reads=[("R4", "w", s)] + [hnk(t, dc, tb2) for dc in range(16)])
                        tr.emit("act", lambda h, b=b, k=k: h.activation(out=rl[k], in_=psum[b][:, :], func=AF.Relu),
                                reads=[("PS", b)], writes=[("S5a", "rl", k)])
                        tr.emit("act", lambda h, k=k: h.activation(out=rl[k], in_=rl[k], func=AF.Square),
                                reads=[("S5a", "rl", k)], writes=[("S5a", "rl", k)])
                        tr.emit("dve", lambda h, k=k, fc=fc, tsl=tsl, t=t, tb2=tb2: h.tensor_tensor(
                            out=rv(t, fc, tsl), in0=rl[k], in1=rs_c[tb2], op=ALU.mult),
                            reads=[("S5a", "rl", k), ("S5", "rsc", tb2)], writes=[rk(t, fc, tb2)])
                def down_group(s, db, tb2):
                    b = next_bank()
                    tsl = slice(tb2 * 512, (tb2 + 1) * 512)
                    mm_group(b, [(psum[b][:, :], wslot[s][:, fc, :], rv(t, fc, tsl), fc == 0, fc == 15)
                                 for fc in range(16)],
                             reads=[("R4", "w", s)] + [rk(t, fc, tb2) for fc in range(16)])
                    tr.emit("dve", lambda h, b=b, db=db, tsl=tsl, t=t: h.tensor_tensor(
                        out=hv(t, db, tsl), in0=psum[b][:, :], in1=hv(t, db, tsl), op=ALU.add),
                        reads=[("PS", b), hk(t, db, tb2)], writes=[hk(t, db, tb2)])

                if t == 1 and q == 3:
                    def last_pass(tb2, hook=None):
                        fbk = next_bank()
                        reserved.add(fbk)
                        pend = None
                        tslf = slice(tb2 * 512, (tb2 + 1) * 512)

                        def f_stat(db, ks):
                            tr.emit("pe", lambda h, ks=ks, db=db, fbk=fbk: h.matmul(
                                psum[fbk][:, :], ones, sqc[ks], start=(db == 0), stop=(db == 15)),
                                reads=[("S5", "sqc", ks), ("C", "ones")], writes=[("PS", fbk)])

                        for db in range(16):
                            s = take_weight()
                            down_group(s, db, tb2)
                            if pend is not None:
                                f_stat(*pend)
                                pend = None
                            if hook is not None and db >= 3:
                                hook(2 if db < 15 else 10 ** 6)
                            ks = sq_ctr[0] % 2
                            sq_ctr[0] += 1
                            tr.emit("act", lambda h, db=db, ks=ks, tslf=tslf: h.activation(
                                out=sqc[ks], in_=hv(1, db, tslf), func=AF.Square),
                                reads=[hk(1, db, tb2)], writes=[("S5", "sqc", ks)])
                            pend = (db, ks)
                        f_stat(*pend)
                        return fbk

                    fb0 = last_pass(0)
                    g10 = [None]

                    def hook10(n):
                        if g10[0] is None:
                            g10[0] = e_half_gen(1, 0, pre_bank=fb0)
                        for _ in range(n):
                            if next(g10[0], "done") == "done":
                                return

                    fb1 = last_pass(1, hook=hook10)
                    e_half(1, 1, pre_bank=fb1)
                elif t == 0 and q == 3:
                    fbt = [next_bank(), next_bank()]
                    reserved.update(fbt)
                    pendq = []

                    def f_stat0(db, tb2, ks):
                        tr.emit("pe", lambda h, ks=ks, db=db, fb=fbt[tb2]: h.matmul(
                            psum[fb][:, :], ones, sqc[ks], start=(db == 0), stop=(db == 15)),
                            reads=[("S5", "sqc", ks), ("C", "ones")], writes=[("PS", fbt[tb2])])

                    for db in range(16):
                        s = take_weight()
                        for tb2 in range(2):
                            down_group(s, db, tb2)
                            if pendq:
                                f_stat0(*pendq.pop(0))
                            ks = sq_ctr[0] % 2
                            sq_ctr[0] += 1
                            tslq = slice(tb2 * 512, (tb2 + 1) * 512)
                            tr.emit("act", lambda h, db=db, ks=ks, tslq=tslq: h.activation(
                                out=sqc[ks], in_=hv(0, db, tslq), func=AF.Square),
                                reads=[hk(0, db, tb2)], writes=[("S5", "sqc", ks)])
                            pendq.append((db, tb2, ks))
                    while pendq:
                        f_stat0(*pendq.pop(0))
                    e0_banks[:] = fbt
                else:
                    for db in range(16):
                        s = take_weight()
                        for tb2 in range(2):
                            down_group(s, db, tb2)
            if t == 0:
                dump("hD", hbuf)
            if t == 0:
                tr.barrier("S5a")
                issue_xr(1, 0)
                issue_xr(1, 1)
                import itertools
                e0_gen[0] = itertools.chain(e_half_gen(0, 0, pre_bank=e0_banks[0]),
                                            e_half_gen(0, 1, pre_bank=e0_banks[1]))

        for j in range(4):
            tr.prog["sp"].append(("wait", f"d_out{j}", tr.count[f"d_out{j}"]))

        _check_no_deadlock(tr)
        semkeys = sorted(tr.count.keys())
        sems = {k: es.enter_context(nc.semaphore(k)) for k in semkeys}

        def replay(eng, h):
            for item in tr.prog[eng]:
                if item[0] == "wait":
                    h.wait_ge(sems[item[1]], item[2])
                else:
                    _, fn, sk, inc = item
                    ins = fn(h)
                    if sk is not None:
                        ins.then_inc(sems[sk], inc)

        with nc.Block() as block:
            @block.sync
            def _(h):
                replay("sp", h)

            @block.gpsimd
            def _(h):
                replay("pool", h)

            @block.tensor
            def _(h):
                replay("pe", h)

            @block.scalar
            def _(h):
                replay("act", h)

            @block.vector
            def _(h):
                replay("dve", h)
    nc._dbg_outs = dbg_outs
    return nc


def _tile_w(w, nblk):
    K, N = w.shape
    j = N // nblk
    t = w.reshape(K // 128, 128, nblk, j).transpose(2, 1, 0, 3)
    return np.ascontiguousarray(t).reshape(nblk, 128, (K // 128) * j)


def _const_tables():
    bf = ml_dtypes.bfloat16
    m = np.arange(128)
    ang = 2.0 * np.pi * ((m[:, None] * m[None, :]) % 128).astype(np.float64) / 128.0
    ccsc = np.zeros((128, 272), dtype=np.float64)
    ccsc[:, 0:128] = np.cos(ang) / 512.0
    ccsc[:, 128:256] = np.sin(ang) / 512.0
    ccsc[:, 256] = 1.0 - 2.0 * (m % 2)
    s = np.arange(S)
    k = np.arange(S // 2)
    angs = 2.0 * np.pi * ((s[:, None] * k[None, :]) % S).astype(np.float64) / S
    tabs = np.stack([np.cos(angs), np.sin(angs)], axis=0)
    tabs = tabs.reshape(2, 16, 128, 2, 512).transpose(3, 2, 0, 1, 4)
    tabs = np.ascontiguousarray(tabs).reshape(2, 128, 2 * 16 * 512)
    return ccsc.astype(np.float32).astype(bf), tabs.astype(np.float32).astype(bf)


_CACHE = {}


def _prep(x, norm_mix_g, w_in, fourier_w, gmlp_v_g, gmlp_ws, gmlp_b, w_out,
          norm_mlp_g, w_up, w_down, norm_final_g):
    x = np.asarray(x, dtype=np.float32)
    f = lambda a: np.asarray(a, dtype=np.float32)
    w_in, w_out, w_up, w_down = f(w_in), f(w_out), f(w_up), f(w_down)
    if "tables" not in _CACHE:
        _CACHE["tables"] = _const_tables()
    ccsc, tabs = _CACHE["tables"]
    wfm = _tile_w(w_in[:, :2048], 16)
    wv = _tile_w(w_in[:, 2048:], 2)
    wout = _tile_w(w_out, 16)
    wup = _tile_w(w_up, 64)
    wdn = np.ascontiguousarray(
        w_down.reshape(4, 16, 128, 16, 128).transpose(0, 3, 2, 1, 4)).reshape(64, 128, 2048)
    cvec = np.concatenate([f(norm_mix_g).reshape(16, 128).T, f(norm_mlp_g).reshape(16, 128).T,
                           f(norm_final_g).reshape(16, 128).T, f(gmlp_v_g).T], axis=1)
    cvec = np.ascontiguousarray(cvec, dtype=np.float32)
    gb = np.ascontiguousarray(f(gmlp_b))
    wsT = np.ascontiguousarray(f(gmlp_ws).transpose(2, 0, 1)).reshape(128, 1024)
    wf = np.ascontiguousarray(f(fourier_w).transpose(1, 0, 2)).reshape(128, 1024)
    shared = {"wfm": wfm, "wv": wv, "wout": wout, "wup": wup, "wdn": wdn, "cvec": cvec, "gb": gb,
              "wsT": wsT, "wf": wf, "ccsc": ccsc, "tabs": tabs}
    in_maps = []
    for b in range(NCORES):
        m = dict(shared)
        m["xT"] = np.ascontiguousarray(x[b].T)
        in_maps.append(m)
    return in_maps


def kernel(x, norm_mix_g, w_in, fourier_w, gmlp_v_g, gmlp_ws, gmlp_b, w_out,
           norm_mlp_g, w_up, w_down, norm_final_g):
    if "nc" not in _CACHE:
        _CACHE["nc"] = build_nc()
    nc = _CACHE["nc"]
    in_maps = _prep(x, norm_mix_g, w_in, fourier_w, gmlp_v_g, gmlp_ws, gmlp_b, w_out,
                    norm_mlp_g, w_up, w_down, norm_final_g)
    res = run_bass_kernel_spmd(nc, in_maps, core_ids=list(range(NCORES)))
    out = np.empty((NCORES, S, D), dtype=np.float32)
    for b in range(NCORES):
        out[b] = np.asarray(res.results[b]["yT"]).T
    return out
```
